# Optimizing a Trainium2 kernel written in Bass

```python
import math
import jax, jax.numpy as jnp
from jax import lax
import numpy as np

D_MODEL = 1024
BATCH = 8
SEQ = 2048
DEPTH = 2

CHUNK = 64
Q_BLOCK = 128
MEM_LEN = 256

A_HEADS = 8
A_HEAD_DIM = 64
A_LEFT_CHUNKS = 8
A_BAND = (A_LEFT_CHUNKS + 1) * CHUNK
A_REL_MIN = -(CHUNK - 1)
A_REL_MAX = 256
A_REL_SIZE = A_REL_MAX - A_REL_MIN + 1

B_HEADS = 8
B_Q_LORA = 384
B_KV_LORA = 256
B_NOPE = 64
B_ROPE = 32
B_V = 64
ROPE_BASE = 10000.0

C_HEADS = 8
C_HEAD_DIM = 64

N_BRANCHES = 3
BRANCH_WIDTH = 512

A_COLS = 3 * A_HEADS * A_HEAD_DIM
B_COLS = B_Q_LORA + B_KV_LORA + B_ROPE
C_COLS = 3 * C_HEADS * C_HEAD_DIM
GATE_COLS = N_BRANCHES * D_MODEL
IN_COLS = A_COLS + B_COLS + C_COLS + C_HEADS + GATE_COLS

XA_HEADS = 4
XA_HEAD_DIM = 128

FFN_HIDDEN = ((8 * D_MODEL // 3 + 255) // 256) * 256

LN_EPS = 1e-5
RMS_EPS = 1e-6

kernel_name = "hybrid_chunk_causal_gated_encoder"


def layer_norm(x, g, b):
    xf = x.astype(jnp.float32)
    mu = jnp.mean(xf, axis=-1, keepdims=True)
    var = jnp.mean(jnp.square(xf - mu), axis=-1, keepdims=True)
    return ((xf - mu) * lax.rsqrt(var + LN_EPS) * g.astype(jnp.float32) + b.astype(jnp.float32)).astype(x.dtype)


def rms_norm(x, g):
    xf = x.astype(jnp.float32)
    ms = jnp.mean(jnp.square(xf), axis=-1, keepdims=True)
    return (xf * lax.rsqrt(ms + RMS_EPS) * g.astype(jnp.float32)).astype(x.dtype)


def rope(x, pos):
    half = x.shape[-1] // 2
    inv_freq = ROPE_BASE ** (-jnp.arange(half, dtype=jnp.float32) / half)
    ang = pos.astype(jnp.float32)[..., None] * inv_freq
    ang = ang.reshape(ang.shape[:2] + (1,) * (x.ndim - 3) + (half,))
    cos, sin = jnp.cos(ang), jnp.sin(ang)
    x1 = x[..., :half].astype(jnp.float32)
    x2 = x[..., half:].astype(jnp.float32)
    return jnp.concatenate([x1 * cos - x2 * sin, x2 * cos + x1 * sin], axis=-1).astype(x.dtype)


def split_cols(h, widths):
    out, off = [], 0
    for w in widths:
        out.append(h[..., off:off + w])
        off += w
    return out


def chunked_relpos_attention(q, k, v, rel_bias):
    B, S, H, Dh = q.shape
    nc = S // CHUNK
    pad = A_LEFT_CHUNKS * CHUNK
    kp = jnp.pad(k, ((0, 0), (pad, 0), (0, 0), (0, 0))).reshape(B, nc + A_LEFT_CHUNKS, CHUNK, H, Dh)
    vp = jnp.pad(v, ((0, 0), (pad, 0), (0, 0), (0, 0))).reshape(B, nc + A_LEFT_CHUNKS, CHUNK, H, Dh)
    k_band = jnp.concatenate([kp[:, j:j + nc] for j in range(A_LEFT_CHUNKS + 1)], axis=2)
    v_band = jnp.concatenate([vp[:, j:j + nc] for j in range(A_LEFT_CHUNKS + 1)], axis=2)
    qc = q.reshape(B, nc, CHUNK, H, Dh)
    s = jnp.einsum('bcqhd,bckhd->bchqk', qc, k_band).astype(jnp.float32) * (Dh ** -0.5)
    qi = jnp.arange(CHUNK)[:, None]
    kj = jnp.arange(A_BAND)[None, :]
    rel = pad + qi - kj
    idx = jnp.clip(rel, A_REL_MIN, A_REL_MAX) - A_REL_MIN
    bias = rel_bias.astype(jnp.float32)[:, idx]
    s = s + bias[None, None]
    chunk_ids = jnp.arange(nc)[:, None]
    valid = (chunk_ids * CHUNK - pad + kj) >= 0
    s = jnp.where(valid[None, :, None, None, :], s, -jnp.inf)
    p = jax.nn.softmax(s, axis=-1).astype(v.dtype)
    o = jnp.einsum('bchqk,bckhd->bcqhd', p, v_band)
    return o.reshape(B, S, H * Dh)


def mla_attention(c_q, c_kv, k_rope_in, pos, q_norm_g, kv_norm_g, w_uq, w_ukv):
    B, S, _ = c_q.shape
    q = (rms_norm(c_q, q_norm_g) @ w_uq).reshape(B, S, B_HEADS, B_NOPE + B_ROPE)
    q_nope = q[..., :B_NOPE]
    q_pe = rope(q[..., B_NOPE:], pos)
    kv = (rms_norm(c_kv, kv_norm_g) @ w_ukv).reshape(B, S, B_HEADS, B_NOPE + B_V)
    k_nope, v = kv[..., :B_NOPE], kv[..., B_NOPE:]
    k_pe = rope(k_rope_in, pos)
    scale = (B_NOPE + B_ROPE) ** -0.5
    key_chunk = jnp.arange(S) // CHUNK

    def block(i):
        start = i * Q_BLOCK
        qn = lax.dynamic_slice_in_dim(q_nope, start, Q_BLOCK, axis=1)
        qp = lax.dynamic_slice_in_dim(q_pe, start, Q_BLOCK, axis=1)
        s = (jnp.einsum('bqhd,bkhd->bhqk', qn, k_nope).astype(jnp.float32)
             + jnp.einsum('bqhr,bkr->bhqk', qp, k_pe).astype(jnp.float32)) * scale
        q_chunk = (start + jnp.arange(Q_BLOCK)) // CHUNK
        mask = key_chunk[None, :] <= q_chunk[:, None]
        s = jnp.where(mask, s, -jnp.inf)
        p = jax.nn.softmax(s, axis=-1).astype(v.dtype)
        return jnp.einsum('bhqk,bkhd->bqhd', p, v)

    o = lax.map(block, jnp.arange(S // Q_BLOCK))
    return o.transpose(1, 0, 2, 3, 4).reshape(B, S, B_HEADS * B_V)


def forgetting_attention(q, k, v, f_logit):
    B, S, H, Dh = q.shape
    log_f = jax.nn.log_sigmoid(f_logit.astype(jnp.float32))
    F = jnp.cumsum(log_f, axis=1).transpose(0, 2, 1)
    scale = Dh ** -0.5
    key_pos = jnp.arange(S)

    def block(i):
        start = i * Q_BLOCK
        qb = lax.dynamic_slice_in_dim(q, start, Q_BLOCK, axis=1)
        Fq = lax.dynamic_slice_in_dim(F, start, Q_BLOCK, axis=2)
        s = jnp.einsum('bqhd,bkhd->bhqk', qb, k).astype(jnp.float32) * scale
        s = s + Fq[..., :, None] - F[..., None, :]
        mask = key_pos[None, :] <= (start + jnp.arange(Q_BLOCK))[:, None]
        s = jnp.where(mask, s, -jnp.inf)
        p = jax.nn.softmax(s, axis=-1).astype(v.dtype)
        return jnp.einsum('bhqk,bkhd->bqhd', p, v)

    o = lax.map(block, jnp.arange(S // Q_BLOCK))
    return o.transpose(1, 0, 2, 3, 4).reshape(B, S, H * Dh)


def hybrid_mixer(x, pos, w_in, b_gate, b_forget, a_rel_bias, b_q_norm, b_kv_norm,
                 b_w_uq, b_w_ukv, w_branch, w_out):
    B, S, _ = x.shape
    h = x @ w_in
    a_qkv, b_cq, b_ckv, b_kr, c_qkv, c_f, g = split_cols(
        h, [A_COLS, B_Q_LORA, B_KV_LORA, B_ROPE, C_COLS, C_HEADS, GATE_COLS])
    a_qkv = a_qkv.reshape(B, S, 3, A_HEADS, A_HEAD_DIM)
    y_a = chunked_relpos_attention(a_qkv[:, :, 0], a_qkv[:, :, 1], a_qkv[:, :, 2], a_rel_bias)
    y_b = mla_attention(b_cq, b_ckv, b_kr, pos, b_q_norm, b_kv_norm, b_w_uq, b_w_ukv)
    c_qkv = c_qkv.reshape(B, S, 3, C_HEADS, C_HEAD_DIM)
    y_c = forgetting_attention(c_qkv[:, :, 0], c_qkv[:, :, 1], c_qkv[:, :, 2], c_f + b_forget)
    branches = jnp.stack([y_a, y_b, y_c], axis=2)
    proj = jnp.einsum('bsnw,nwd->bsnd', branches, w_branch)
    gates = jax.nn.sigmoid(g.reshape(B, S, N_BRANCHES, D_MODEL) + b_gate)
    merged = jnp.sum(gates * proj, axis=2)
    return merged @ w_out


def memory_cross_attention(x, mem, w_q, w_kv, w_o):
    B, S, _ = x.shape
    M = mem.shape[1]
    q = (x @ w_q).reshape(B, S, XA_HEADS, XA_HEAD_DIM)
    kv = (mem @ w_kv).reshape(B, M, 2, XA_HEADS, XA_HEAD_DIM)
    s = jnp.einsum('bqhd,bkhd->bhqk', q, kv[:, :, 0]).astype(jnp.float32) * (XA_HEAD_DIM ** -0.5)
    p = jax.nn.softmax(s, axis=-1).astype(x.dtype)
    o = jnp.einsum('bhqk,bkhd->bqhd', p, kv[:, :, 1]).reshape(B, S, XA_HEADS * XA_HEAD_DIM)
    return o @ w_o


def swiglu_ffn(x, w_gu, w_down):
    gu = x @ w_gu
    g, u = gu[..., :FFN_HIDDEN], gu[..., FFN_HIDDEN:]
    return (jax.nn.silu(g) * u) @ w_down


def setup_inputs(seed: int = 0) -> dict:
    key = jax.random.key(seed)
    ks = jax.random.split(key, 24)
    f32 = jnp.float32
    L = DEPTH
    beta = (8 * DEPTH) ** -0.25

    def w(k, shape, fan_in, scale=1.0):
        return jax.random.normal(k, shape, f32) * (scale * fan_in ** -0.5)

    def gain(k, shape):
        return 1.0 + 0.05 * jax.random.normal(k, shape, f32)

    def small(k, shape, s=0.02):
        return s * jax.random.normal(k, shape, f32)

    x = jax.random.normal(ks[0], (BATCH, SEQ, D_MODEL), f32)
    mem = jax.random.normal(ks[1], (BATCH, MEM_LEN, D_MODEL), f32)
    offs = jax.random.randint(ks[2], (BATCH, 1), 0, 16) * CHUNK
    positions = (jnp.arange(SEQ, dtype=jnp.int32)[None, :] + offs).astype(jnp.int32)

    return {
        "x": x,
        "mem": mem,
        "positions": positions,
        "ln_mix_g": gain(ks[3], (L, D_MODEL)),
        "ln_mix_b": small(ks[4], (L, D_MODEL)),
        "w_in": w(ks[5], (L, D_MODEL, IN_COLS), D_MODEL),
        "b_gate": small(ks[6], (L, N_BRANCHES, D_MODEL)),
        "b_forget": 3.0 + 0.5 * jax.random.normal(ks[7], (L, C_HEADS), f32),
        "a_rel_bias": small(ks[8], (L, A_HEADS, A_REL_SIZE), 0.5),
        "b_q_norm": gain(ks[9], (L, B_Q_LORA)),
        "b_kv_norm": gain(ks[10], (L, B_KV_LORA)),
        "b_w_uq": w(ks[11], (L, B_Q_LORA, B_HEADS * (B_NOPE + B_ROPE)), B_Q_LORA),
        "b_w_ukv": w(ks[12], (L, B_KV_LORA, B_HEADS * (B_NOPE + B_V)), B_KV_LORA),
        "w_branch": w(ks[13], (L, N_BRANCHES, BRANCH_WIDTH, D_MODEL), BRANCH_WIDTH),
        "w_mix_out": w(ks[14], (L, D_MODEL, D_MODEL), D_MODEL, beta),
        "ln_xa_g": gain(ks[15], (L, D_MODEL)),
        "ln_xa_b": small(ks[16], (L, D_MODEL)),
        "xa_w_q": w(ks[17], (L, D_MODEL, XA_HEADS * XA_HEAD_DIM), D_MODEL),
        "xa_w_kv": w(ks[18], (L, D_MODEL, 2 * XA_HEADS * XA_HEAD_DIM), D_MODEL),
        "xa_w_o": w(ks[19], (L, XA_HEADS * XA_HEAD_DIM, D_MODEL), XA_HEADS * XA_HEAD_DIM, beta),
        "ln_ffn_g": gain(ks[20], (L, D_MODEL)),
        "ln_ffn_b": small(ks[21], (L, D_MODEL)),
        "ffn_w_gu": w(ks[22], (L, D_MODEL, 2 * FFN_HIDDEN), D_MODEL),
        "ffn_w_down": w(ks[23], (L, FFN_HIDDEN, D_MODEL), FFN_HIDDEN, beta),
    }


def reference(x, mem, positions, ln_mix_g, ln_mix_b, w_in, b_gate, b_forget, a_rel_bias,
              b_q_norm, b_kv_norm, b_w_uq, b_w_ukv, w_branch, w_mix_out, ln_xa_g, ln_xa_b,
              xa_w_q, xa_w_kv, xa_w_o, ln_ffn_g, ln_ffn_b, ffn_w_gu, ffn_w_down):
    alpha = (2 * DEPTH) ** 0.25
    for l in range(DEPTH):
        y = hybrid_mixer(x, positions, w_in[l], b_gate[l], b_forget[l], a_rel_bias[l],
                         b_q_norm[l], b_kv_norm[l], b_w_uq[l], b_w_ukv[l], w_branch[l], w_mix_out[l])
        x = layer_norm(alpha * x + y, ln_mix_g[l], ln_mix_b[l])
        y = memory_cross_attention(x, mem, xa_w_q[l], xa_w_kv[l], xa_w_o[l])
        x = layer_norm(alpha * x + y, ln_xa_g[l], ln_xa_b[l])
        y = swiglu_ffn(x, ffn_w_gu[l], ffn_w_down[l])
        x = layer_norm(alpha * x + y, ln_ffn_g[l], ln_ffn_b[l])
    return x
```

```python
import math
from contextlib import ExitStack

import numpy as np
import concourse.bass as bass
import concourse.mybir as mybir
from concourse.bass_utils import run_bass_kernel_spmd

F32 = mybir.dt.float32
BF16 = mybir.dt.bfloat16
I32 = mybir.dt.int32
AF = mybir.ActivationFunctionType
ALU = mybir.AluOpType

T = 2048
D = 1024
NTB = 16
NTT = 4
DEPTH = 2
ALPHA = float((2 * DEPTH) ** 0.25)
IN_COLS = 6824
FFN_H = 2816
NJ = 22
LN_EPS = 1e-5
RMS_EPS = 1e-6
NEG = -30000.0

SAME_ENG_SYNC = True
N_DMA_SEMS = 24


class Prog:
    def __init__(self, nc):
        self.nc = nc
        self.ops = []
        self.res = {}
        self.n_dma = 0
        self.slot_last = {}
        self.phase_key = ("__phase__",)
        self.last_on_eng = {}

    def _st(self, k):
        st = self.res.get(k)
        if st is None:
            st = {"w": [], "r": [], "pw": [], "pr": []}
            self.res[k] = st
        return st

    def op(self, eng, fn, reads=(), writes=(), pwrites=(), dma=False):
        i = len(self.ops)
        deps = set()
        deps.update(self._st(self.phase_key)["w"])
        for r in reads:
            st = self._st(r)
            deps.update(st["w"])
            st["r"].append(i)
        for w in writes:
            st = self._st(w)
            deps.update(st["w"])
            deps.update(st["r"])
            st["w"] = [i]
            st["r"] = []
            st["pw"] = []
            st["pr"] = []
        for w in pwrites:
            st = self._st(w)
            if st["r"]:
                st["pw"], st["pr"] = st["w"], st["r"]
                st["w"], st["r"] = [], []
            deps.update(st["pw"])
            deps.update(st["pr"])
            st["w"].append(i)
        slot = None
        if dma:
            slot = self.n_dma % N_DMA_SEMS
            self.n_dma += 1
            if slot in self.slot_last:
                deps.add(self.slot_last[slot])
            self.slot_last[slot] = i
        deps.discard(i)
        if not dma:
            self.last_on_eng[eng] = i
        self.ops.append({"eng": eng, "fn": fn, "deps": deps, "dma": dma, "slot": slot})
        return i

    def barrier(self, fn):
        i = len(self.ops)
        st = self._st(self.phase_key)
        deps = set(st["w"]) | set(self.last_on_eng.values()) | set(self.slot_last.values())
        st["w"] = [i]
        st["r"] = []
        self.ops.append({"eng": "pool", "fn": fn, "deps": deps, "dma": False, "slot": None, "bar": True})

    def setup(self, sems, dma_sems):
        nc = self.nc
        self.sems = sems
        self.dma_sems = dma_sems
        self.engobj = {"pe": nc.tensor, "act": nc.scalar, "dve": nc.vector, "pool": nc.gpsimd, "sp": nc.sync}
        self.cnt = {e: 0 for e in self.engobj}
        self.dcnt = [0] * len(dma_sems)
        self.sig = []
        self.seen = {e: {} for e in self.engobj}
        self.start = 0
        self.last_bar = -1

    def flush(self):
        ops = self.ops
        start = self.start
        engobj = self.engobj

        def skip(d, e):
            if d < start and d != self.last_bar:
                return True
            od = ops[d]
            return (not od["dma"]) and od["eng"] == e and (e == "pe" or not SAME_ENG_SYNC)

        has_dep = {}
        for j in range(start, len(ops)):
            o = ops[j]
            for d in o["deps"]:
                if not skip(d, o["eng"]):
                    has_dep[d] = True
        self.sig.extend([None] * (len(ops) - len(self.sig)))
        sig = self.sig
        for j in range(start, len(ops)):
            o = ops[j]
            e = o["eng"]
            eo = engobj[e]
            tgt = {}
            for d in o["deps"]:
                if skip(d, e):
                    continue
                s = sig[d]
                assert s is not None, (j, d)
                if tgt.get(s[0], (None, 0))[1] < s[2]:
                    tgt[s[0]] = (s[1], s[2])
            for key, (sem, val) in tgt.items():
                if self.seen[e].get(key, 0) >= val:
                    continue
                eo.wait_ge(sem, val)
                self.seen[e][key] = val
            ins = o["fn"]()
            if o["dma"]:
                k = o["slot"]
                self.dcnt[k] += 16
                ins.then_inc(self.dma_sems[k], 16)
                sig[j] = (("d", k), self.dma_sems[k], self.dcnt[k])
            elif has_dep.get(j) or o.get("bar"):
                self.cnt[e] += 1
                ins.then_inc(self.sems[e], 1)
                sig[j] = (("e", e), self.sems[e], self.cnt[e])
            o["fn"] = None
            if o.get("bar"):
                self.last_bar = j
        self.start = len(ops)


def fm(W):
    K, n = W.shape
    return np.ascontiguousarray(W.reshape(K // 128, 128, n).transpose(1, 0, 2)).reshape(128, -1)


def _const_block():
    c = np.zeros((128, 385), np.float32)
    c[:, 0:128] = np.eye(128, dtype=np.float32)
    for k in range(64):
        c[k, 128 + 64 + k] = 1.0
    kk = np.arange(128)[:, None]
    qq = np.arange(128)[None, :]
    c[:, 256:384] = (kk <= qq).astype(np.float32)
    half = 16
    inv = (np.float32(10000.0) ** (-(np.arange(half, dtype=np.float32)) / np.float32(half))).astype(np.float32)
    c[64:96, 384] = np.concatenate([inv, inv])
    return c


def _a_bias_idx():
    k = np.arange(128)[:, None]
    col = np.arange(640)[None, :]
    i = col // 128
    qp = col % 128
    rel = i * 128 + qp - k
    ck = k // 64
    cq = 2 * i + qp // 64
    valid = (ck <= cq) & (ck >= cq - 8)
    idx = np.clip(rel, -63, 256) + 63
    return idx, valid


def specs():
    S = []

    def add(name, n, f):
        S.append((name, n, f))

    add("const", 385, lambda I: _const_block())
    idx, valid = _a_bias_idx()
    for l in range(DEPTH):
        def W(I, l=l):
            return I["w_in"][l]
        for h in range(8):
            cols = np.concatenate([np.arange(h * 64, h * 64 + 64), 512 + np.arange(h * 64, h * 64 + 64),
                                   1024 + np.arange(h * 64, h * 64 + 64)])
            add(f"A_qkv_{l}_{h}", 8 * 192, lambda I, cols=cols, W=W: fm(W(I)[:, cols]))
            add(f"A_bias_{l}_{h}", 640,
                lambda I, l=l, h=h: np.where(valid, I["a_rel_bias"][l, h][idx], np.float32(NEG)).astype(np.float32))
        add(f"B_cq_{l}", 8 * 384, lambda I, W=W: fm(W(I)[:, 1536:1920]))
        add(f"B_ckv_{l}", 8 * 256, lambda I, W=W: fm(W(I)[:, 1920:2176]))
        krc = np.concatenate([2176 + np.arange(32), 2176 + np.arange(32), 2176 + np.arange(32), 2176 + 16 + np.arange(16), 2176 + np.arange(16)])
        add(f"B_kr_{l}", 8 * 128, lambda I, W=W, krc=krc: fm(W(I)[:, krc]))
        for h in range(8):
            b = h * 96
            cq = np.concatenate([b + np.arange(64), b + 64 + np.arange(32), b + 64 + 16 + np.arange(16), b + 64 + np.arange(16)])
            add(f"B_uq_{l}_{h}", 3 * 128, lambda I, l=l, cq=cq: fm(I["b_w_uq"][l][:, cq]))
            b2 = h * 128
            ck = b2 + np.arange(64)
            add(f"B_uk_{l}_{h}", 2 * 64, lambda I, l=l, ck=ck: fm(I["b_w_ukv"][l][:, ck]))
        cv = np.concatenate([h * 128 + 64 + np.arange(64) for h in range(8)])
        add(f"B_uv_{l}", 2 * 512, lambda I, l=l, cv=cv: fm(I["b_w_ukv"][l][:, cv]))
        add(f"B_gq_{l}", 3, lambda I, l=l: np.ascontiguousarray(I["b_q_norm"][l].reshape(3, 128).T))
        add(f"B_gkv_{l}", 2, lambda I, l=l: np.ascontiguousarray(I["b_kv_norm"][l].reshape(2, 128).T))
        for h in range(8):
            cols = np.concatenate([2208 + np.arange(h * 64, h * 64 + 64), 2208 + 512 + np.arange(h * 64, h * 64 + 64),
                                   2208 + 1024 + np.arange(h * 64, h * 64 + 64)])
            add(f"C_qkv_{l}_{h}", 8 * 192, lambda I, cols=cols, W=W: fm(W(I)[:, cols]))
        add(f"C_f_{l}", 64, lambda I, W=W: fm(W(I)[:, 3744:3752]))

        def bf_blk(I, l=l):
            o = np.zeros((128, 1), np.float32)
            o[0:8, 0] = I["b_forget"][l]
            return o
        add(f"C_bf_{l}", 1, bf_blk)
        for d in range(8):
            for n in range(3):
                add(f"M_{l}_{d}_{n}", 1536,
                    lambda I, l=l, d=d, n=n, W=W: np.concatenate(
                        [fm(W(I)[:, 3752 + n * 1024 + d * 128: 3752 + n * 1024 + d * 128 + 128]),
                         fm(I["w_branch"][l, n][:, d * 128:(d + 1) * 128])], axis=1))
        add(f"M_bg_{l}", 24, lambda I, l=l: np.ascontiguousarray(
            I["b_gate"][l].reshape(3, 8, 128).transpose(2, 0, 1)).reshape(128, 24))
        add(f"M_out_{l}", 8192, lambda I, l=l: fm(I["w_mix_out"][l]))
        for h in range(4):
            add(f"X_q_{l}_{h}", 1024, lambda I, l=l, h=h: fm(I["xa_w_q"][l][:, h * 128:(h + 1) * 128]))
            add(f"X_k_{l}_{h}", 1024, lambda I, l=l, h=h: fm(I["xa_w_kv"][l][:, h * 128:(h + 1) * 128]))
        add(f"X_v_{l}", 4096, lambda I, l=l: fm(I["xa_w_kv"][l][:, 512:1024]))
        add(f"X_o_{l}", 4096, lambda I, l=l: fm(I["xa_w_o"][l]))
        for j in range(NJ):
            cols = np.concatenate([j * 128 + np.arange(128), FFN_H + j * 128 + np.arange(128)])
            add(f"F_gu_{l}_{j}", 2048, lambda I, l=l, cols=cols: fm(I["ffn_w_gu"][l][:, cols]))
        for jj in range(11):
            add(f"F_d_{l}_{jj}", 2048, lambda I, l=l, jj=jj: fm(I["ffn_w_down"][l][jj * 256:(jj + 1) * 256, :]))
        for nm in ("mix", "xa", "ffn"):
            add(f"LN_{nm}_g_{l}", 1024, lambda I, l=l, nm=nm: np.broadcast_to(I[f"ln_{nm}_g"][l][None, :], (128, 1024)))
            add(f"LN_{nm}_b_{l}", 1024, lambda I, l=l, nm=nm: np.broadcast_to(I[f"ln_{nm}_b"][l][None, :], (128, 1024)))
    return S


_SPECS = None
_OFF = None
_NW = 0


def layout():
    global _SPECS, _OFF, _NW
    if _SPECS is None:
        _SPECS = specs()
        _OFF = {}
        o = 0
        for name, n, f in _SPECS:
            _OFF[name] = (o, n)
            o += n
        _NW = o
    return _SPECS, _OFF, _NW


def pack_weights(inputs):
    S, OFF, NW = layout()
    I = {k: np.asarray(v) for k, v in inputs.items()}
    wp = np.empty((128, NW), np.float32)
    for name, n, f in S:
        o = OFF[name][0]
        wp[:, o:o + n] = f(I)
    return wp


def build(stage=None, dbg_shape=None):
    S, OFF, NW = layout()
    nc = bass.Bass("TRN2", target_bir_lowering=False)
    xin = nc.dram_tensor("xin", [T, D], F32, kind="ExternalInput").ap()
    memin = nc.dram_tensor("mem", [256, D], F32, kind="ExternalInput").ap()
    posin = nc.dram_tensor("pos", [1, T], I32, kind="ExternalInput").ap()
    wpk = nc.dram_tensor("wpack", [128, NW], F32, kind="ExternalInput").ap()
    outd = nc.dram_tensor("out", [T, D], F32, kind="ExternalOutput").ap()
    xres = [nc.dram_tensor(f"xres{i}", [T, D], F32, kind="Internal").ap() for i in range(3)]
    dbg = None
    if dbg_shape is not None:
        dbg = nc.dram_tensor("dbg", list(dbg_shape), F32, kind="ExternalOutput").ap()

    es = ExitStack()
    with es:
        P = Prog(nc)
        sems = {e: es.enter_context(nc.semaphore("s_" + e)) for e in ["pe", "act", "dve", "pool", "sp"]}
        dsems = [es.enter_context(nc.semaphore("d%d" % i)) for i in range(N_DMA_SEMS)]
        P.setup(sems, dsems)

        uid = [0]

        def sb(st, name, shape, dt):
            uid[0] += 1
            return st.enter_context(nc.sbuf_tensor(f"{name}_u{uid[0]}", shape, dt))

        xT = sb(es, "xT", [128, 8, T], BF16)
        stg = [sb(es, f"stg{i}", [128, 2048], F32) for i in range(2)]
        ident = sb(es, "ident", [128, 128], BF16)
        shiftI = sb(es, "shiftI", [64, 128], BF16)
        cmask = sb(es, "cmask", [128, 128], BF16)
        invf = sb(es, "invf", [96, 1], F32)
        ones_f = sb(es, "ones_f", [128, 128], F32)
        ones_b = sb(es, "ones_b", [128, 128], BF16)
        memT = sb(es, "memT", [128, 8, 256], BF16)
        cosT = sb(es, "cosT", [96, T], F32)
        sinT = sb(es, "sinT", [96, T], F32)
        lng = sb(es, "lng", [128, 1024], F32)
        lnb = sb(es, "lnb", [128, 1024], F32)
        xbuf = [sb(es, f"xbuf{i}", [128, 1024], F32) for i in range(2)]
        xb16 = [sb(es, f"xb16{i}", [128, 1024], BF16) for i in range(2)]
        lnst = [sb(es, f"lnst{i}", [128, 24], F32) for i in range(2)]
        dummy = sb(es, "dummy", [128, 8], F32)
        psS = [es.enter_context(nc.psum_tensor(f"psS{i}", [128, 512], F32)) for i in range(3)]
        psO = [es.enter_context(nc.psum_tensor(f"psO{i}", [128, 512], F32)) for i in range(2)]
        psP = [es.enter_context(nc.psum_tensor(f"psP{i}", [128, 512], F32)) for i in range(2)]
        psT = es.enter_context(nc.psum_tensor("psT", [128, 1024], BF16))

        ctr = {"stg": 0, "p": 0, "s": 0, "o": 0, "xb": 0, "pt": 0}

        def nextP():
            i = ctr["p"] % 2
            ctr["p"] += 1
            return psP[i], ("psP", i)

        def barrier():
            P.barrier(lambda: nc.gpsimd.memset(dummy[:, 0:1], 0.0))
            P.flush()

        def wload(name, dst, dkey, lo=0, n=None, cast=None, pw=True, srcview=None):
            off, tot = OFF[name]
            if n is None:
                n = tot
            assert n <= 2048 and lo + n <= tot
            i = ctr["stg"] % 2
            ctr["stg"] += 1
            sk = ("stg", i)
            st = stg[i]
            P.op("sp", lambda: nc.sync.dma_start(out=st[:, 0:n], in_=wpk[:, off + lo: off + lo + n]), writes=[sk], dma=True)
            if cast is None:
                src = st[:, 0:n] if srcview is None else srcview(st[:, 0:n])
                P.op("pool", lambda: nc.gpsimd.tensor_copy(out=dst, in_=src), reads=[sk],
                     **({"pwrites": [dkey]} if pw else {"writes": [dkey]}))
            else:
                cast(st, sk)

        def vload(name, dst, dkey, eng="sp"):
            off, tot = OFF[name]
            P.op("sp", lambda: nc.sync.dma_start(out=dst, in_=wpk[:, off: off + tot]), writes=[dkey], dma=True)

        def transposes_to_xT(src16, skey, tb):
            for c in range(8):
                P.op("pe", lambda c=c: nc.tensor.transpose(out=psT[:, c * 128:(c + 1) * 128], in_=src16[:, c * 128:(c + 1) * 128],
                                                          identity=ident[:]), reads=[skey, "ident"], pwrites=["psT"])
            P.op("dve", lambda: nc.vector.tensor_copy(out=xT[:, :, tb * 128:(tb + 1) * 128],
                                                      in_=psT[:, :].rearrange("p (c t) -> p c t", c=8)),
                 reads=["psT"], pwrites=[("xT", tb // 4)])

        def ln_block(tb, ybanks, xsrc, xdst):
            i = ctr["xb"] % 2
            ctr["xb"] += 1
            xb, xk = xbuf[i], ("xbuf", i)
            x16, x16k = xb16[i], ("xb16", i)
            st, stk = lnst[i], ("lnst", i)
            xsrc_ap, xsrc_n = xsrc
            xdst_ap, xdst_n = xdst
            P.op("sp", lambda: nc.sync.dma_start(out=xb[:], in_=xsrc_ap[tb * 128:(tb + 1) * 128, :]), reads=[("xd", xsrc_n, tb)], writes=[xk], dma=True)
            for hf in range(2):
                yb, yk = ybanks[hf]
                P.op("dve", lambda hf=hf, yb=yb: nc.vector.scalar_tensor_tensor(
                    out=xb[:, hf * 512:(hf + 1) * 512], in0=xb[:, hf * 512:(hf + 1) * 512], scalar=ALPHA, in1=yb,
                    op0=ALU.mult, op1=ALU.add), reads=[yk, xk], writes=[xk])
            for hf in range(2):
                P.op("dve", lambda hf=hf: nc.vector.bn_stats(out=st[:, hf * 6:(hf + 1) * 6], in_=xb[:, hf * 512:(hf + 1) * 512]),
                     reads=[xk], pwrites=[stk])
            P.op("dve", lambda: nc.vector.bn_aggr(out=st[:, 12:14], in_=st[:, 0:12]), reads=[stk], writes=[stk])
            P.op("dve", lambda: nc.vector.tensor_scalar(out=st[:, 14:15], in0=st[:, 13:14], scalar1=LN_EPS, scalar2=None, op0=ALU.add),
                 reads=[stk], writes=[stk])
            P.op("act", lambda: nc.scalar.activation(out=st[:, 14:15], in_=st[:, 14:15], func=AF.Sqrt), reads=[stk], writes=[stk])
            P.op("dve", lambda: nc.vector.reciprocal(out=st[:, 15:16], in_=st[:, 14:15]), reads=[stk], writes=[stk])
            P.op("dve", lambda: nc.vector.scalar_tensor_tensor(out=st[:, 16:17], in0=st[:, 12:13], scalar=-1.0, in1=st[:, 15:16],
                                                               op0=ALU.mult, op1=ALU.mult), reads=[stk], writes=[stk])
            P.op("act", lambda: nc.scalar.activation(out=xb[:], in_=xb[:], func=AF.Identity, scale=st[:, 15:16], bias=st[:, 16:17]),
                 reads=[stk, xk], writes=[xk])
            P.op("pool", lambda: nc.gpsimd.tensor_tensor(out=xb[:], in0=xb[:], in1=lng[:], op=ALU.mult), reads=[xk, "lng"], writes=[xk])
            P.op("dve", lambda: nc.vector.tensor_tensor(out=xb[:], in0=xb[:], in1=lnb[:], op=ALU.add), reads=[xk, "lnb"], writes=[xk])
            P.op("act", lambda: nc.scalar.copy(out=x16[:], in_=xb[:]), reads=[xk], writes=[x16k])
            P.op("sp", lambda: nc.sync.dma_start(out=xdst_ap[tb * 128:(tb + 1) * 128, :], in_=xb[:]), reads=[xk],
                 writes=[("xd", xdst_n, tb)], dma=True)
            transposes_to_xT(x16, x16k, tb)

        cst = stg[0]
        off_c = OFF["const"][0]
        P.op("sp", lambda: nc.sync.dma_start(out=cst[:, 0:385], in_=wpk[:, off_c: off_c + 385]), writes=[("stg", 0)], dma=True)
        P.op("pool", lambda: nc.gpsimd.tensor_copy(out=ident[:], in_=cst[:, 0:128]), reads=[("stg", 0)], writes=["ident"])
        P.op("pool", lambda: nc.gpsimd.tensor_copy(out=shiftI[:], in_=cst[0:64, 128:256]), reads=[("stg", 0)], writes=["shiftI"])
        P.op("pool", lambda: nc.gpsimd.tensor_copy(out=cmask[:], in_=cst[:, 256:384]), reads=[("stg", 0)], writes=["cmask"])
        P.op("pool", lambda: nc.gpsimd.tensor_copy(out=invf[64:96, :], in_=cst[64:96, 384:385]), reads=[("stg", 0)], writes=["invf"])
        P.op("pool", lambda: nc.gpsimd.memset(ones_f[:], 1.0), writes=["ones_f"])
        P.op("pool", lambda: nc.gpsimd.memset(ones_b[:], 1.0), writes=["ones_b"])
        ctr["stg"] = 1
        with ExitStack() as ph:
            posi_ = sb(ph, "posi", [96, T], I32)
            ang_ = sb(ph, "ang", [96, T], F32)
            kf_ = sb(ph, "kf", [96, T], F32)
            ki_ = sb(ph, "ki", [96, T], I32)
            r2_ = sb(ph, "r2", [96, T], F32)
            posi, ang, kf, ki, r2 = posi_[64:96, :], ang_[64:96, :], kf_[64:96, :], ki_[64:96, :], r2_[64:96, :]
            cosT_, sinT_ = cosT[64:96, :], sinT[64:96, :]
            P.op("sp", lambda: nc.sync.dma_start(out=posi, in_=posin.partition_broadcast(32)), writes=["posi"], dma=True)
            P.op("dve", lambda: nc.vector.tensor_copy(out=ang, in_=posi), reads=["posi"], writes=["ang"])
            P.op("dve", lambda: nc.vector.tensor_scalar(out=ang, in0=ang, scalar1=invf[64:96, 0:1], scalar2=None, op0=ALU.mult),
                 reads=["ang", "invf"], writes=["ang"])

            def reduce_sin(src, skey, dst, dkey, shift):
                P.op("dve", lambda: nc.vector.tensor_scalar(out=r2, in0=src, scalar1=shift, scalar2=None, op0=ALU.add),
                     reads=[skey], writes=["r2"])
                P.op("dve", lambda: nc.vector.tensor_scalar(out=kf, in0=r2, scalar1=1.0 / (2 * math.pi), scalar2=None, op0=ALU.mult),
                     reads=["r2"], writes=["kf"])
                P.op("dve", lambda: nc.vector.tensor_copy(out=ki, in_=kf), reads=["kf"], writes=["ki"])
                P.op("dve", lambda: nc.vector.tensor_copy(out=kf, in_=ki), reads=["ki"], writes=["kf"])
                C1 = 6.28125
                C2 = 2 * math.pi - 6.28125
                P.op("dve", lambda: nc.vector.scalar_tensor_tensor(out=r2, in0=kf, scalar=-C1, in1=r2, op0=ALU.mult, op1=ALU.add),
                     reads=["kf", "r2"], writes=["r2"])
                P.op("dve", lambda: nc.vector.scalar_tensor_tensor(out=r2, in0=kf, scalar=-C2, in1=r2, op0=ALU.mult, op1=ALU.add),
                     reads=["kf", "r2"], writes=["r2"])
                P.op("dve", lambda: nc.vector.tensor_scalar(out=kf, in0=r2, scalar1=0.0, scalar2=2 * math.pi, op0=ALU.is_lt, op1=ALU.mult),
                     reads=["r2"], writes=["kf"])
                P.op("dve", lambda: nc.vector.tensor_tensor(out=r2, in0=r2, in1=kf, op=ALU.add), reads=["kf", "r2"], writes=["r2"])
                P.op("dve", lambda: nc.vector.tensor_scalar(out=kf, in0=r2, scalar1=2 * math.pi, scalar2=-2 * math.pi, op0=ALU.is_ge, op1=ALU.mult),
                     reads=["r2"], writes=["kf"])
                P.op("dve", lambda: nc.vector.tensor_tensor(out=r2, in0=r2, in1=kf, op=ALU.add), reads=["kf", "r2"], writes=["r2"])
                P.op("dve", lambda: nc.vector.tensor_scalar(out=r2, in0=r2, scalar1=0.0, scalar2=2 * math.pi, op0=ALU.max, op1=ALU.min),
                     reads=["r2"], writes=["r2"])
                P.op("act", lambda: nc.scalar.activation(out=dst, in_=r2, func=AF.Sin, scale=-1.0, bias=math.pi),
                     reads=["r2"], writes=[dkey])

            reduce_sin(ang, "ang", sinT_, "sinT", 0.0)
            reduce_sin(ang, "ang", cosT_, "cosT", math.pi / 2)
            barrier()
        for mb in range(2):
            i = ctr["xb"] % 2
            ctr["xb"] += 1
            xb, xk = xbuf[i], ("xbuf", i)
            x16, x16k = xb16[i], ("xb16", i)
            P.op("sp", lambda mb=mb, xb=xb: nc.sync.dma_start(out=xb[:], in_=memin[mb * 128:(mb + 1) * 128, :]), writes=[xk], dma=True)
            P.op("act", lambda xb=xb, x16=x16: nc.scalar.copy(out=x16[:], in_=xb[:]), reads=[xk], writes=[x16k])
            for c in range(8):
                P.op("pe", lambda c=c, x16=x16: nc.tensor.transpose(out=psT[:, c * 128:(c + 1) * 128], in_=x16[:, c * 128:(c + 1) * 128],
                                                                   identity=ident[:]), reads=[x16k, "ident"], pwrites=["psT"])
            P.op("dve", lambda mb=mb: nc.vector.tensor_copy(out=memT[:, :, mb * 128:(mb + 1) * 128],
                                                           in_=psT[:, :].rearrange("p (c t) -> p c t", c=8)),
                 reads=["psT"], pwrites=["memT"])
        for tb in range(NTB):
            i = ctr["xb"] % 2
            ctr["xb"] += 1
            xb, xk = xbuf[i], ("xbuf", i)
            x16, x16k = xb16[i], ("xb16", i)
            P.op("sp", lambda tb=tb, xb=xb: nc.sync.dma_start(out=xb[:], in_=xin[tb * 128:(tb + 1) * 128, :]), writes=[xk], dma=True)
            P.op("act", lambda xb=xb, x16=x16: nc.scalar.copy(out=x16[:], in_=xb[:]), reads=[xk], writes=[x16k])
            transposes_to_xT(x16, x16k, tb)

        def proj_fm(wb, wkey, ncc, col0, M, src, srckeyf, evac, mrow0=0):
            for tt in range(NTT):
                pb, pk = nextP()
                for c in range(ncc):
                    P.op("pe", lambda c=c, tt=tt, pb=pb: nc.tensor.matmul(pb[0:M, :], lhsT=wb[:, c, col0:col0 + M],
                                                                          rhs=src[:, c, tt * 512:(tt + 1) * 512],
                                                                          start=(c == 0), stop=(c == ncc - 1)),
                         reads=[wkey, srckeyf(tt)], pwrites=[pk])
                evac(tt, pb, pk)

        def attention(n, h, qT, qk, kT, kk, V, vk, Kd, kind, biasb=None, bk=None, pTs=None, tmps=None, rc=None, bcs=None, tmpo=None, ybr=None):
            steps = []
            for qt in range(NTT):
                if kind == "A":
                    kbs = list(range(max(0, 4 * qt - 4), 4 * qt + 4))
                else:
                    kbs = list(range(0, 4 * qt + 4))
                for ii, kb in enumerate(kbs):
                    if kind == "A":
                        jlo = max(kb, 4 * qt)
                        jhi = min(kb + 4, 4 * qt + 3)
                    else:
                        jlo = max(kb, 4 * qt)
                        jhi = 4 * qt + 3
                    c0 = (jlo - 4 * qt) * 128
                    c1 = (jhi - 4 * qt + 1) * 128
                    steps.append(dict(qt=qt, kb=kb, c0=c0, c1=c1, first=(ii == 0), last=(ii == len(kbs) - 1),
                                      diag=(kb >= 4 * qt), b0=(jlo - kb) * 128))
            nS = len(steps)
            sbank = {}
            obank = {}
            pending = []

            def emit_S(i):
                s = steps[i]
                bi = ctr["s"] % 3
                ctr["s"] += 1
                sbank[i] = bi
                q0 = s["qt"] * 512
                P.op("pe", lambda s=s, bi=bi: nc.tensor.matmul(psS[bi][:, s["c0"]:s["c1"]], lhsT=kT[0:Kd, s["kb"] * 128:(s["kb"] + 1) * 128],
                                                               rhs=qT[0:Kd, q0 + s["c0"]: q0 + s["c1"]], start=True, stop=True),
                     reads=[kk, qk], pwrites=[("psS", bi)])

            def emit_exp(i):
                s = steps[i]
                bi = sbank[i]
                pi = ctr["pt"] % len(pTs)
                ctr["pt"] += 1
                s["pi"] = pi
                pT = pTs[pi]
                pk = ("pT", pi)
                c0, c1 = s["c0"], s["c1"]
                if kind == "A":
                    ti = pi % len(tmps)
                    tm = tmps[ti]
                    P.op("dve", lambda: nc.vector.tensor_tensor(out=tm[:, c0:c1], in0=psS[bi][:, c0:c1], in1=biasb[:, s["b0"]: s["b0"] + c1 - c0],
                                                                op=ALU.add), reads=[("psS", bi), bk], writes=[("tmp", ti)])
                    P.op("act", lambda: nc.scalar.activation(out=pT[:, c0:c1], in_=tm[:, c0:c1], func=AF.Exp), reads=[("tmp", ti)], writes=[pk])
                else:
                    P.op("act", lambda: nc.scalar.activation(out=pT[:, c0:c1], in_=psS[bi][:, c0:c1], func=AF.Exp), reads=[("psS", bi)], writes=[pk])
                    if s["diag"]:
                        if kind == "B":
                            P.op("pool", lambda: nc.gpsimd.memset(pT[64:128, c0:c0 + 64], 0.0), reads=[pk], writes=[pk])
                        elif kind == "C":
                            P.op("pool", lambda: nc.gpsimd.tensor_tensor(out=pT[:, c0:c0 + 128], in0=pT[:, c0:c0 + 128], in1=cmask[:],
                                                                         op=ALU.mult), reads=[pk, "cmask"], writes=[pk])

            def emit_PV(i):
                s = steps[i]
                if s["first"]:
                    oi = ctr["o"] % 2
                    ctr["o"] += 1
                    obank[s["qt"]] = oi
                oi = obank[s["qt"]]
                pT = pTs[s["pi"]]
                c0, c1 = s["c0"], s["c1"]
                P.op("pe", lambda: nc.tensor.matmul(psO[oi][0:65, c0:c1], lhsT=V[:, s["kb"], 0:65], rhs=pT[:, c0:c1],
                                                    start=s["first"], stop=s["last"], skip_group_check=True),
                     reads=[vk, ("pT", s["pi"])], pwrites=[("psO", oi)])
                if s["last"]:
                    qt = s["qt"]
                    ok = ("psO", oi)
                    P.op("dve", lambda: nc.vector.reciprocal(out=rc[64:65, :], in_=psO[oi][64:65, :]), reads=[ok], writes=["rc"])

                    def norm():
                        pb, pk = nextP()
                        P.op("pe", lambda: nc.tensor.matmul(pb[:, :], lhsT=ones_f[64:65, 0:128], rhs=rc[64:65, :], start=True, stop=True),
                             reads=["rc", "ones_f"], pwrites=[pk])
                        bi = ctr.get("bc", 0) % len(bcs)
                        ctr["bc"] = ctr.get("bc", 0) + 1
                        bc = bcs[bi]
                        P.op("dve", lambda: nc.vector.tensor_copy(out=bc[0:64, :], in_=pb[0:64, :]), reads=[pk], writes=[("bc", bi)])
                        cc = h // 2
                        if h % 2 == 0:
                            P.op("dve", lambda: nc.vector.tensor_tensor(out=ybr[0:64, cc, qt * 512:(qt + 1) * 512], in0=psO[oi][0:64, :],
                                                                        in1=bc[0:64, :], op=ALU.mult),
                                 reads=[ok, ("bc", bi)], pwrites=[("y", n, qt)])
                        else:
                            P.op("dve", lambda: nc.vector.tensor_tensor(out=tmpo[0:64, :], in0=psO[oi][0:64, :], in1=bc[0:64, :], op=ALU.mult),
                                 reads=[ok, ("bc", bi)], writes=["tmpo"])
                            pb2, pk2 = nextP()
                            P.op("pe", lambda: nc.tensor.matmul(pb2[:, :], lhsT=shiftI[0:64, :], rhs=tmpo[0:64, :], start=True, stop=True),
                                 reads=["tmpo", "shiftI"], pwrites=[pk2])
                            P.op("dve", lambda: nc.vector.tensor_copy(out=ybr[64:128, cc, qt * 512:(qt + 1) * 512], in_=pb2[64:128, :]),
                                 reads=[pk2], pwrites=[("y", n, qt)])
                    pending.append((i + 3, norm))

            LA = 2
            for i in range(min(LA, nS)):
                emit_S(i)
            for i in range(nS):
                while pending and pending[0][0] <= i:
                    pending.pop(0)[1]()
                if i + LA < nS:
                    emit_S(i + LA)
                emit_exp(i)
                emit_PV(i)
            while pending:
                pending.pop(0)[1]()

        for l in range(DEPTH):
            xsrc = (xin, "xin") if l == 0 else (xres[2], "xres2")
            with ExitStack() as mx:
                ybrs = [sb(mx, f"ybr{n}", [128, 4, T], BF16) for n in range(3)]
                def plain_mixer(n, kind):
                    with ExitStack() as ph:
                        Kd = 64
                        biasbs = None
                        tmps = None
                        if kind == "C":
                            Kd = 70
                            Fp = [sb(ph, f"Fp{i}", [8, T], BF16) for i in range(3)]
                            Fn = [sb(ph, f"Fn{i}", [8, T], BF16) for i in range(3)]
                            with ExitStack() as pp_:
                                wf = sb(pp_, "wf", [128, 8, 8], BF16)
                                bfg = sb(pp_, "bfg", [8, 1], F32)
                                lf = sb(pp_, "lf", [8, T], F32)
                                Ff = sb(pp_, "Ff", [8, T], F32)
                                wload(f"C_f_{l}", wf[:, :, :].rearrange("p c j -> p (c j)"), "wf", pw=False)
                                offb = OFF[f"C_bf_{l}"][0]
                                P.op("sp", lambda: nc.sync.dma_start(out=bfg[:], in_=wpk[0:8, offb:offb + 1], allow_slow_non_contiguous=True), writes=["bfg"], dma=True)
                                P.op("dve", lambda: nc.vector.tensor_scalar(out=bfg[:], in0=bfg[:], scalar1=-1.0, scalar2=None, op0=ALU.mult),
                                     reads=["bfg"], writes=["bfg"])

                                def evac_f(tt, pb, pk):
                                    P.op("act", lambda: nc.scalar.activation(out=lf[:, tt * 512:(tt + 1) * 512], in_=pb[0:8, :], func=AF.Exp,
                                                                             scale=-1.0, bias=bfg[:, 0:1]), reads=[pk, "bfg"], pwrites=["lf"])
                                proj_fm(wf, "wf", 8, 0, 8, xT, lambda tt: ("xT", tt), evac_f)
                                P.op("act", lambda: nc.scalar.activation(out=lf[:], in_=lf[:], func=AF.Ln, bias=1.0, scale=1.0), reads=["lf"], writes=["lf"])
                                P.op("dve", lambda: nc.vector.tensor_tensor_scan(out=Ff[:], data0=ones_f[0:8, 0:1].to_broadcast([8, T]), data1=lf[:],
                                                                                 initial=0.0, op0=ALU.mult, op1=ALU.subtract),
                                     reads=["lf", "ones_f"], writes=["Ff"])
                                for i3 in range(3):
                                    P.op("dve", lambda i3=i3: nc.vector.tensor_copy(out=Fp[i3][:], in_=Ff[:]), reads=["Ff"], writes=[("Fp", i3)])
                                    P.op("dve", lambda i3=i3: nc.vector.tensor_scalar(out=Fn[i3][:], in0=Fp[i3][:], scalar1=-1.0, scalar2=None, op0=ALU.mult),
                                         reads=[("Fp", i3)], writes=[("Fn", i3)])
                                    if i3 < 2:
                                        P.op("dve", lambda i3=i3: nc.vector.tensor_tensor(out=Ff[:], in0=Ff[:], in1=Fp[i3][:], op=ALU.subtract),
                                             reads=["Ff", ("Fp", i3)], writes=["Ff"])
                                barrier()
                        wbs = [sb(ph, f"wb{i}", [128, 8, 192], BF16) for i in range(2)]
                        qTs = [sb(ph, f"qT{i}", [128, T], BF16) for i in range(2)]
                        kTs = [sb(ph, f"kT{i}", [128, T], BF16) for i in range(2)]
                        Vs = [sb(ph, f"V{i}", [128, NTB, 65], BF16) for i in range(2)]
                        pTs = [sb(ph, f"pT{i}", [128, 512], BF16) for i in range(4)]
                        rc = sb(ph, "rc", [128, 512], F32)
                        bcs = [sb(ph, f"bc{i}", [64, 512], F32) for i in range(2)]
                        tmpo = sb(ph, "tmpo", [64, 512], BF16)
                        if kind == "A":
                            biasbs = [sb(ph, f"biasb{i}", [128, 640], F32) for i in range(2)]
                            tmps = [sb(ph, f"tmp{i}", [128, 512], F32) for i in range(2)]
                        else:
                            for i2 in range(2):
                                P.op("pool", lambda i2=i2: nc.gpsimd.memset(qTs[i2][64:70, :], 1.0), writes=[("q", i2)])
                                P.op("pool", lambda i2=i2: nc.gpsimd.memset(kTs[i2][64:70, :], 1.0), writes=[("k", i2)])
                        for i2 in range(2):
                            P.op("pool", lambda i2=i2: nc.gpsimd.memset(Vs[i2][:, :, 64:65], 1.0), writes=[("v", i2)])
                        pref = "A" if kind == "A" else "C"

                        def load_head(h):
                            i = h % 2
                            wload(f"{pref}_qkv_{l}_{h}", wbs[i][:, :, :].rearrange("p c j -> p (c j)"), ("wb", i), pw=False)
                            if kind == "A":
                                vload(f"A_bias_{l}_{h}", biasbs[i][:], ("biasb", i))
                        load_head(0)
                        for h in range(8):
                            i = h % 2
                            if h + 1 < 8:
                                load_head(h + 1)
                            wb, wk = wbs[i], ("wb", i)
                            qT, qk = qTs[i], ("q", i)
                            kT, kk = kTs[i], ("k", i)
                            V, vk = Vs[i], ("v", i)

                            def evq(tt, pb, pk, qT=qT, qk=qk):
                                P.op("dve", lambda: nc.vector.tensor_scalar(out=qT[0:64, tt * 512:(tt + 1) * 512], in0=pb[0:64, :], scalar1=0.125,
                                                                            scalar2=None, op0=ALU.mult), reads=[pk], pwrites=[qk])

                            def evk(tt, pb, pk, kT=kT, kk=kk):
                                P.op("act", lambda: nc.scalar.copy(out=kT[0:64, tt * 512:(tt + 1) * 512], in_=pb[0:64, :]), reads=[pk], pwrites=[kk])
                            proj_fm(wb, wk, 8, 0, 64, xT, lambda tt: ("xT", tt), evq)
                            proj_fm(wb, wk, 8, 64, 64, xT, lambda tt: ("xT", tt), evk)
                            for g in range(2):
                                pb, pk = nextP()
                                for t8 in range(8):
                                    tb = g * 8 + t8
                                    for c in range(8):
                                        P.op("pe", lambda c=c, tb=tb, t8=t8, pb=pb, wb=wb: nc.tensor.matmul(
                                            pb[:, t8 * 64:(t8 + 1) * 64], lhsT=xT[:, c, tb * 128:(tb + 1) * 128], rhs=wb[:, c, 128:192],
                                            start=(c == 0), stop=(c == 7)), reads=[wk, ("xT", tb // 4)], pwrites=[pk])
                                P.op("dve", lambda g=g, pb=pb, V=V: nc.vector.tensor_copy(
                                    out=V[:, g * 8:(g + 1) * 8, 0:64], in_=pb[:, :].rearrange("p (t d) -> p t d", t=8)), reads=[pk], pwrites=[vk])
                            if kind == "C":
                                for i3 in range(3):
                                    P.op("sp", lambda i3=i3, qT=qT, h=h: nc.sync.dma_start(out=qT[64 + i3:65 + i3, :], in_=Fp[i3][h:h + 1, :]),
                                         reads=[("Fp", i3)], pwrites=[qk], dma=True)
                                    P.op("sp", lambda i3=i3, kT=kT, h=h: nc.sync.dma_start(out=kT[67 + i3:68 + i3, :], in_=Fn[i3][h:h + 1, :]),
                                         reads=[("Fn", i3)], pwrites=[kk], dma=True)
                            attention(n, h, qT, qk, kT, kk, V, vk, Kd, kind, biasb=(biasbs[i] if biasbs else None), bk=("biasb", i),
                                      pTs=pTs, tmps=tmps, rc=rc, bcs=bcs, tmpo=tmpo, ybr=ybrs[n])
                        barrier()

                def mla_mixer(n):
                    with ExitStack() as ph:
                        cqT = sb(ph, "cqT", [128, 3, T], BF16)
                        ckvT = sb(ph, "ckvT", [128, 2, T], BF16)
                        Vall = sb(ph, "Vall", [128, NTB, 8, 65], BF16)
                        kTs = [sb(ph, f"kT{i}", [128, T], BF16) for i in range(2)]
                        gq = sb(ph, "gq", [128, 3], F32)
                        gkv = sb(ph, "gkv", [128, 2], F32)
                        vload(f"B_gq_{l}", gq[:], "gq")
                        vload(f"B_gkv_{l}", gkv[:], "gkv")
                        P.op("pool", lambda: nc.gpsimd.memset(Vall[:, :, :, 64:65], 1.0), writes=["vall"])
                        with ExitStack() as pp_:
                            wbig = sb(pp_, "wbig", [128, 8, 384], BF16)
                            sq = sb(pp_, "sq", [128, 3, 512], F32)
                            rs = sb(pp_, "rs", [128, 512], F32)
                            wkr = sb(pp_, "wkr", [128, 8, 128], BF16)
                            wuv = sb(pp_, "wuv", [128, 2, 512], BF16)
                            rt = [sb(pp_, f"rt{i}", [96, 512], F32) for i in range(2)]

                            def latent(name, ncol, nch, dstT, dkey, dim, scale):
                                for half in range(2):
                                    wload(name, wbig[:, half * 4:(half + 1) * 4, 0:ncol], "wbig", lo=half * 4 * ncol, n=4 * ncol,
                                          srcview=lambda a: a.rearrange("p (c j) -> p c j", c=4))
                                for tt in range(NTT):
                                    sl = slice(tt * 512, (tt + 1) * 512)
                                    for j in range(nch):
                                        for c in range(8):
                                            P.op("pe", lambda c=c, j=j, sl=sl: nc.tensor.matmul(
                                                psS[j][:, :], lhsT=wbig[:, c, j * 128:(j + 1) * 128], rhs=xT[:, c, sl],
                                                start=(c == 0), stop=(c == 7)), reads=["wbig", ("xT", tt)], pwrites=[("psS", j)])
                                        P.op("act", lambda j=j: nc.scalar.activation(out=sq[:, j, :], in_=psS[j][:, :], func=AF.Square),
                                             reads=[("psS", j)], pwrites=["sq"])
                                    pb, pk = nextP()
                                    for j in range(nch):
                                        P.op("pe", lambda j=j, pb=pb: nc.tensor.matmul(pb[:, :], lhsT=ones_f[:, :], rhs=sq[:, j, :],
                                                                                       start=(j == 0), stop=(j == nch - 1)),
                                             reads=["sq", "ones_f"], pwrites=[pk])
                                    P.op("dve", lambda pb=pb: nc.vector.tensor_scalar(out=rs[:, :], in0=pb[:, :], scalar1=1.0 / dim, scalar2=RMS_EPS,
                                                                                      op0=ALU.mult, op1=ALU.add), reads=[pk], writes=["rs"])
                                    P.op("act", lambda: nc.scalar.activation(out=rs[:, :], in_=rs[:, :], func=AF.Sqrt), reads=["rs"], writes=["rs"])
                                    P.op("dve", lambda: nc.vector.reciprocal(out=rs[:, :], in_=rs[:, :]), reads=["rs"], writes=["rs"])
                                    if scale != 1.0:
                                        P.op("dve", lambda: nc.vector.tensor_scalar(out=rs[:, :], in0=rs[:, :], scalar1=scale, scalar2=None, op0=ALU.mult),
                                             reads=["rs"], writes=["rs"])
                                    for j in range(nch):
                                        P.op("dve", lambda j=j, sl=sl: nc.vector.tensor_tensor(out=dstT[:, j, sl], in0=psS[j][:, :], in1=rs[:, :], op=ALU.mult),
                                             reads=[("psS", j), "rs"], pwrites=[dkey])

                            latent(f"B_cq_{l}", 384, 3, cqT, "cqT", 384.0, 96.0 ** -0.5)
                            latent(f"B_ckv_{l}", 256, 2, ckvT, "ckvT", 256.0, 1.0)

                            def cast_kr(st, sk):
                                v = st[:, 0:1024].rearrange("p (c j) -> p c j", c=8)
                                P.op("pool", lambda: nc.gpsimd.tensor_copy(out=wkr[:, :, 0:96], in_=v[:, :, 0:96]), reads=[sk], pwrites=["wkr"])
                                P.op("pool", lambda: nc.gpsimd.tensor_scalar(out=wkr[:, :, 96:112], in0=v[:, :, 96:112], scalar1=-1.0, scalar2=None, op0=ALU.mult),
                                     reads=[sk], pwrites=["wkr"])
                                P.op("pool", lambda: nc.gpsimd.tensor_copy(out=wkr[:, :, 112:128], in_=v[:, :, 112:128]), reads=[sk], pwrites=["wkr"])
                            wload(f"B_kr_{l}", None, None, cast=cast_kr)
                            for tt in range(NTT):
                                sl = slice(tt * 512, (tt + 1) * 512)
                                pb, pk = nextP()
                                pb2, pk2 = nextP()
                                for c in range(8):
                                    P.op("pe", lambda c=c, sl=sl, pb=pb: nc.tensor.matmul(pb[0:96, :], lhsT=wkr[:, c, 0:96], rhs=xT[:, c, sl],
                                                                                          start=(c == 0), stop=(c == 7)), reads=["wkr", ("xT", tt)], pwrites=[pk])
                                for c in range(8):
                                    P.op("pe", lambda c=c, sl=sl, pb2=pb2: nc.tensor.matmul(pb2[0:96, :], lhsT=wkr[:, c, 32:128], rhs=xT[:, c, sl],
                                                                                            start=(c == 0), stop=(c == 7)), reads=["wkr", ("xT", tt)], pwrites=[pk2])
                                P.op("dve", lambda sl=sl, pb=pb: nc.vector.tensor_tensor(out=rt[0][64:96, :], in0=pb[64:96, :], in1=cosT[64:96, sl], op=ALU.mult),
                                     reads=[pk, "cosT"], writes=[("rt", 0)])
                                P.op("dve", lambda sl=sl, pb2=pb2: nc.vector.tensor_tensor(out=rt[1][64:96, :], in0=pb2[64:96, :], in1=sinT[64:96, sl], op=ALU.mult),
                                     reads=[pk2, "sinT"], writes=[("rt", 1)])
                                for i2 in range(2):
                                    P.op("dve", lambda sl=sl, i2=i2: nc.vector.tensor_tensor(out=kTs[i2][64:96, sl], in0=rt[0][64:96, :], in1=rt[1][64:96, :], op=ALU.add),
                                         reads=[("rt", 0), ("rt", 1)], pwrites=[("k", i2)])

                            def cast_uv(st, sk):
                                for j in range(2):
                                    P.op("pool", lambda j=j: nc.gpsimd.tensor_scalar(out=wuv[:, j, :], in0=st[:, j * 512:(j + 1) * 512], scalar1=gkv[:, j:j + 1],
                                                                                     scalar2=None, op0=ALU.mult), reads=[sk, "gkv"], pwrites=["wuv"])
                            wload(f"B_uv_{l}", None, None, cast=cast_uv)
                            for tb in range(NTB):
                                pb, pk = nextP()
                                for j in range(2):
                                    P.op("pe", lambda j=j, tb=tb, pb=pb: nc.tensor.matmul(pb[:, :], lhsT=ckvT[:, j, tb * 128:(tb + 1) * 128], rhs=wuv[:, j, :],
                                                                                          start=(j == 0), stop=(j == 1)), reads=["ckvT", "wuv"], pwrites=[pk])
                                P.op("act", lambda tb=tb, pb=pb: nc.scalar.copy(out=Vall[:, tb, :, 0:64], in_=pb[:, :].rearrange("p (h d) -> p h d", h=8)),
                                     reads=[pk], pwrites=["vall"])
                            barrier()
                        qTs = [sb(ph, f"qT{i}", [128, T], BF16) for i in range(2)]
                        pTs = [sb(ph, f"pT{i}", [128, 512], BF16) for i in range(3)]
                        rc = sb(ph, "rc", [128, 512], F32)
                        bcs = [sb(ph, f"bc{i}", [64, 512], F32) for i in range(1)]
                        tmpo = sb(ph, "tmpo", [64, 512], BF16)
                        wuq = [sb(ph, f"wuq{i}", [128, 3, 128], BF16) for i in range(2)]
                        wuk = [sb(ph, f"wuk{i}", [128, 2, 64], BF16) for i in range(2)]
                        rt = [sb(ph, f"rtb{i}", [96, 512], F32) for i in range(2)]

                        def load_head(h):
                            i = h % 2

                            def cast_uq(st, sk, i=i):
                                v = st[:, 0:384].rearrange("p (c j) -> p c j", c=3)
                                for j in range(3):
                                    P.op("pool", lambda j=j: nc.gpsimd.tensor_scalar(out=wuq[i][:, j, :], in0=v[:, j, :], scalar1=gq[:, j:j + 1],
                                                                                     scalar2=None, op0=ALU.mult), reads=[sk, "gq"], pwrites=[("wuq", i)])
                                P.op("pool", lambda: nc.gpsimd.tensor_scalar(out=wuq[i][:, :, 96:112], in0=wuq[i][:, :, 96:112], scalar1=-1.0, scalar2=None,
                                                                             op0=ALU.mult), reads=[("wuq", i)], writes=[("wuq", i)])

                            def cast_uk(st, sk, i=i):
                                v = st[:, 0:128].rearrange("p (c j) -> p c j", c=2)
                                for j in range(2):
                                    P.op("pool", lambda j=j: nc.gpsimd.tensor_scalar(out=wuk[i][:, j, :], in0=v[:, j, :], scalar1=gkv[:, j:j + 1],
                                                                                     scalar2=None, op0=ALU.mult), reads=[sk, "gkv"], pwrites=[("wuk", i)])
                            wload(f"B_uq_{l}_{h}", None, None, cast=cast_uq)
                            wload(f"B_uk_{l}_{h}", None, None, cast=cast_uk)
                        load_head(0)
                        for h in range(8):
                            i = h % 2
                            if h + 1 < 8:
                                load_head(h + 1)
                            qT, qk = qTs[i], ("q", i)
                            kT, kk = kTs[i], ("k", i)
                            wq, wqk = wuq[i], ("wuq", i)
                            wk_, wkk = wuk[i], ("wuk", i)
                            for tt in range(NTT):
                                sl = slice(tt * 512, (tt + 1) * 512)
                                pb, pk = nextP()
                                pb2, pk2 = nextP()
                                for j in range(3):
                                    P.op("pe", lambda j=j, sl=sl, pb=pb, wq=wq: nc.tensor.matmul(pb[0:96, :], lhsT=wq[:, j, 0:96], rhs=cqT[:, j, sl],
                                                                                                 start=(j == 0), stop=(j == 2)), reads=[wqk, "cqT"], pwrites=[pk])
                                for j in range(3):
                                    P.op("pe", lambda j=j, sl=sl, pb2=pb2, wq=wq: nc.tensor.matmul(pb2[0:96, :], lhsT=wq[:, j, 32:128], rhs=cqT[:, j, sl],
                                                                                                   start=(j == 0), stop=(j == 2)), reads=[wqk, "cqT"], pwrites=[pk2])
                                P.op("act", lambda pb=pb, qT=qT, sl=sl: nc.scalar.copy(out=qT[0:64, sl], in_=pb[0:64, :]), reads=[pk], pwrites=[qk])
                                P.op("dve", lambda pb=pb, sl=sl: nc.vector.tensor_tensor(out=rt[0][64:96, :], in0=pb[64:96, :], in1=cosT[64:96, sl], op=ALU.mult),
                                     reads=[pk, "cosT"], writes=[("rt", 0)])
                                P.op("dve", lambda pb2=pb2, sl=sl: nc.vector.tensor_tensor(out=rt[1][64:96, :], in0=pb2[64:96, :], in1=sinT[64:96, sl], op=ALU.mult),
                                     reads=[pk2, "sinT"], writes=[("rt", 1)])
                                P.op("dve", lambda qT=qT, sl=sl: nc.vector.tensor_tensor(out=qT[64:96, sl], in0=rt[0][64:96, :], in1=rt[1][64:96, :], op=ALU.add),
                                     reads=[("rt", 0), ("rt", 1)], pwrites=[qk])
                            for tt in range(NTT):
                                pb, pk = nextP()
                                sl = slice(tt * 512, (tt + 1) * 512)
                                for j in range(2):
                                    P.op("pe", lambda j=j, pb=pb, wk_=wk_, sl=sl: nc.tensor.matmul(pb[0:64, :], lhsT=wk_[:, j, 0:64], rhs=ckvT[:, j, sl],
                                                                                                   start=(j == 0), stop=(j == 1)), reads=[wkk, "ckvT"], pwrites=[pk])
                                P.op("act", lambda pb=pb, kT=kT, sl=sl: nc.scalar.copy(out=kT[0:64, sl], in_=pb[0:64, :]), reads=[pk], pwrites=[kk])
                            attention(n, h, qT, qk, kT, kk, Vall[:, :, h, :], "vall", 96, "B", pTs=pTs, rc=rc, bcs=bcs, tmpo=tmpo, ybr=ybrs[n])
                        barrier()

                if stage is None or stage >= 1:
                    plain_mixer(0, "A")
                if stage is None or stage >= 2:
                    mla_mixer(1)
                if stage is None or stage >= 3:
                    plain_mixer(2, "C")
                if stage is not None and stage <= 3:
                    for n in range(stage):
                        for cc in range(4):
                            for q4 in range(2):
                                i = ctr["xb"] % 2
                                ctr["xb"] += 1
                                xb, xk = xbuf[i], ("xbuf", i)
                                P.op("dve", lambda xb=xb, n=n, cc=cc, q4=q4: nc.vector.tensor_copy(out=xb[:, :], in_=ybrs[n][:, cc, q4 * 1024:(q4 + 1) * 1024]),
                                     reads=[("y", n, 2 * q4), ("y", n, 2 * q4 + 1)], writes=[xk])
                                P.op("sp", lambda xb=xb, n=n, cc=cc, q4=q4: nc.sync.dma_start(out=dbg[n, cc, :, q4 * 1024:(q4 + 1) * 1024], in_=xb[:, :]),
                                     reads=[xk], dma=True)
                    barrier()
                    break
                with ExitStack() as ph:
                    mergedT = sb(ph, "mergedT", [128, 8, T], BF16)
                    wm = [sb(ph, f"wm{i}", [128, 1536], BF16) for i in range(2)]
                    bg = sb(ph, "bg", [128, 24], F32)
                    macc = sb(ph, "macc", [128, T], F32)
                    sig = [sb(ph, f"sig{i}", [128, 512], F32) for i in range(2)]
                    wout = sb(ph, "wout", [128, 8, 1024], BF16)
                    vload(f"M_bg_{l}", bg[:], "bg")
                    vload(f"LN_mix_g_{l}", lng[:], "lng")
                    vload(f"LN_mix_b_{l}", lnb[:], "lnb")
                    cnt_m = 0
                    order = [(d, n) for d in range(8) for n in range(3)]
                    wload(f"M_{l}_0_0", wm[0][:, :], ("wm", 0), pw=False)
                    for oi_, (d, n) in enumerate(order):
                        i = oi_ % 2
                        if oi_ + 1 < len(order):
                            d2, n2 = order[oi_ + 1]
                            wload(f"M_{l}_{d2}_{n2}", wm[(oi_ + 1) % 2][:, :], ("wm", (oi_ + 1) % 2), pw=False)
                        w = wm[i]
                        wkey = ("wm", i)
                        wg = w[:, 0:1024].rearrange("p (c j) -> p c j", c=8)
                        wbr = w[:, 1024:1536].rearrange("p (c j) -> p c j", c=4)
                        for tt in range(NTT):
                            sl = slice(tt * 512, (tt + 1) * 512)
                            pg, pgk = nextP()
                            pp, ppk = nextP()
                            for c in range(8):
                                P.op("pe", lambda c=c, pg=pg, wg=wg, sl=sl: nc.tensor.matmul(pg[:, :], lhsT=wg[:, c, :], rhs=xT[:, c, sl], start=(c == 0), stop=(c == 7)),
                                     reads=[wkey, ("xT", tt)], pwrites=[pgk])
                            for c in range(4):
                                P.op("pe", lambda c=c, pp=pp, wbr=wbr, sl=sl, n=n: nc.tensor.matmul(pp[:, :], lhsT=wbr[:, c, :], rhs=ybrs[n][:, c, sl], start=(c == 0), stop=(c == 3)),
                                     reads=[wkey, ("y", n, tt)], pwrites=[ppk])
                            si = cnt_m % 2
                            cnt_m += 1
                            sg = sig[si]
                            P.op("act", lambda pg=pg, sg=sg, n=n, d=d: nc.scalar.activation(out=sg[:, :], in_=pg[:, :], func=AF.Sigmoid, bias=bg[:, n * 8 + d: n * 8 + d + 1], scale=1.0),
                                 reads=[pgk, "bg"], writes=[("sig", si)])
                            if n == 0:
                                P.op("dve", lambda pp=pp, sg=sg, sl=sl: nc.vector.tensor_tensor(out=macc[:, sl], in0=pp[:, :], in1=sg[:, :], op=ALU.mult),
                                     reads=[ppk, ("sig", si)], writes=[("macc", tt)])
                            else:
                                P.op("dve", lambda pp=pp, sg=sg: nc.vector.tensor_tensor(out=sg[:, :], in0=pp[:, :], in1=sg[:, :], op=ALU.mult),
                                     reads=[ppk, ("sig", si)], writes=[("sig", si)])
                                if n == 1:
                                    P.op("pool", lambda sg=sg, sl=sl: nc.gpsimd.tensor_tensor(out=macc[:, sl], in0=macc[:, sl], in1=sg[:, :], op=ALU.add),
                                         reads=[("sig", si), ("macc", tt)], writes=[("macc", tt)])
                                else:
                                    P.op("pool", lambda sg=sg, sl=sl, d=d: nc.gpsimd.tensor_tensor(out=mergedT[:, d, sl], in0=macc[:, sl], in1=sg[:, :], op=ALU.add),
                                         reads=[("sig", si), ("macc", tt)], pwrites=[("mT", tt)])
                    for pc in range(4):
                        wload(f"M_out_{l}", wout[:, pc * 2:(pc + 1) * 2, :].rearrange("p c j -> p (c j)"), "wout", lo=pc * 2048, n=2048)
                    for tb in range(NTB):
                        yb = []
                        for hf in range(2):
                            pb, pk = (psS[hf], ("psS", hf)) if tb % 2 == 0 else (psO[hf], ("psO", hf))
                            for c in range(8):
                                P.op("pe", lambda c=c, pb=pb, tb=tb, hf=hf: nc.tensor.matmul(pb[:, :], lhsT=mergedT[:, c, tb * 128:(tb + 1) * 128], rhs=wout[:, c, hf * 512:(hf + 1) * 512],
                                                                                             start=(c == 0), stop=(c == 7)), reads=[("mT", tb // 4), "wout"], pwrites=[pk])
                            yb.append((pb[:, :], pk))
                        ln_block(tb, yb, xsrc, (xres[0], "xres0"))
                    barrier()
            if stage is not None and stage <= 3:
                break
            if stage == 4:
                for tb in range(NTB):
                    P.op("sp", lambda tb=tb: nc.sync.dma_start(out=dbg[tb * 128:(tb + 1) * 128, :], in_=xres[0][tb * 128:(tb + 1) * 128, :]),
                         reads=[("xd", "xres0", tb)], dma=True)
                break
            with ExitStack() as ph:
                wq = [sb(ph, f"wq{i}", [128, 8, 128], BF16) for i in range(2)]
                wk = [sb(ph, f"wk{i}", [128, 8, 128], BF16) for i in range(2)]
                wv = sb(ph, "wv", [128, 8, 512], BF16)
                wo = sb(ph, "wo", [128, 4, 1024], BF16)
                kTx = sb(ph, "kTx", [128, 4, 256], BF16)
                Vx = sb(ph, "Vx", [128, 2, 512], BF16)
                qTx = [sb(ph, f"qTx{i}", [128, T], BF16) for i in range(2)]
                oT = sb(ph, "oT", [128, 4, T], BF16)
                pTs = [sb(ph, f"pT{i}", [128, 512], BF16) for i in range(4)]
                rcx = [sb(ph, f"rcx{i}", [128, 512], F32) for i in range(2)]
                vload(f"LN_xa_g_{l}", lng[:], "lng")
                vload(f"LN_xa_b_{l}", lnb[:], "lnb")
                for pc in range(2):
                    wload(f"X_v_{l}", wv[:, pc * 4:(pc + 1) * 4, :].rearrange("p c j -> p (c j)"), "wv", lo=pc * 2048, n=2048)
                for mb in range(2):
                    pb, pk = nextP()
                    for c in range(8):
                        P.op("pe", lambda c=c, mb=mb, pb=pb: nc.tensor.matmul(pb[:, :], lhsT=memT[:, c, mb * 128:(mb + 1) * 128], rhs=wv[:, c, :], start=(c == 0), stop=(c == 7)),
                             reads=["memT", "wv"], pwrites=[pk])
                    P.op("act", lambda mb=mb, pb=pb: nc.scalar.copy(out=Vx[:, mb, :], in_=pb[:, :]), reads=[pk], pwrites=["Vx"])

                def load_head(h):
                    i = h % 2
                    wload(f"X_q_{l}_{h}", wq[i][:, :, :].rearrange("p c j -> p (c j)"), ("wq", i), pw=False)
                    wload(f"X_k_{l}_{h}", wk[i][:, :, :].rearrange("p c j -> p (c j)"), ("wk", i), pw=False)
                load_head(0)
                for h in range(4):
                    i = h % 2
                    if h + 1 < 4:
                        load_head(h + 1)
                    pb, pk = nextP()
                    for c in range(8):
                        P.op("pe", lambda c=c, pb=pb, i=i: nc.tensor.matmul(pb[:, 0:256], lhsT=wk[i][:, c, :], rhs=memT[:, c, :], start=(c == 0), stop=(c == 7)),
                             reads=[("wk", i), "memT"], pwrites=[pk])
                    P.op("act", lambda pb=pb, h=h: nc.scalar.copy(out=kTx[:, h, :], in_=pb[:, 0:256]), reads=[pk], pwrites=[("kTx", h)])
                    qT, qk = qTx[i], ("qx", i)

                    def evq(tt, pb, pk, qT=qT, qk=qk):
                        P.op("dve", lambda: nc.vector.tensor_scalar(out=qT[:, tt * 512:(tt + 1) * 512], in0=pb[:, :], scalar1=128.0 ** -0.5, scalar2=None, op0=ALU.mult),
                             reads=[pk], pwrites=[qk])
                    proj_fm(wq[i], ("wq", i), 8, 0, 128, xT, lambda tt: ("xT", tt), evq)
                    for qt in range(NTT):
                        sl = slice(qt * 512, (qt + 1) * 512)
                        pis = []
                        for kb in range(2):
                            bi = ctr["s"] % 3
                            ctr["s"] += 1
                            P.op("pe", lambda kb=kb, bi=bi, sl=sl, h=h, qT=qT: nc.tensor.matmul(psS[bi][:, :], lhsT=kTx[:, h, kb * 128:(kb + 1) * 128], rhs=qT[:, sl], start=True, stop=True),
                                 reads=[("kTx", h), qk], pwrites=[("psS", bi)])
                            pi = ctr["pt"] % 4
                            ctr["pt"] += 1
                            pis.append(pi)
                            P.op("act", lambda bi=bi, pi=pi: nc.scalar.activation(out=pTs[pi][:, :], in_=psS[bi][:, :], func=AF.Exp), reads=[("psS", bi)], writes=[("pT", pi)])
                        oi = ctr["o"] % 2
                        ctr["o"] += 1
                        pb, pk = nextP()
                        for kb in range(2):
                            P.op("pe", lambda kb=kb, oi=oi, h=h, pi=pis[kb]: nc.tensor.matmul(psO[oi][:, :], lhsT=Vx[:, kb, h * 128:(h + 1) * 128], rhs=pTs[pi][:, :], start=(kb == 0), stop=(kb == 1)),
                                 reads=["Vx", ("pT", pis[kb])], pwrites=[("psO", oi)])
                        for kb in range(2):
                            P.op("pe", lambda kb=kb, pb=pb, pi=pis[kb]: nc.tensor.matmul(pb[:, :], lhsT=ones_b[:, :], rhs=pTs[pi][:, :], start=(kb == 0), stop=(kb == 1)),
                                 reads=["ones_b", ("pT", pis[kb])], pwrites=[pk])
                        ri = (h * 4 + qt) % 2
                        P.op("dve", lambda pb=pb, ri=ri: nc.vector.reciprocal(out=rcx[ri][:, :], in_=pb[:, :]), reads=[pk], writes=[("rcx", ri)])
                        P.op("dve", lambda oi=oi, ri=ri, h=h, sl=sl: nc.vector.tensor_tensor(out=oT[:, h, sl], in0=psO[oi][:, :], in1=rcx[ri][:, :], op=ALU.mult),
                             reads=[("psO", oi), ("rcx", ri)], pwrites=[("oT", qt)])
                for pc in range(2):
                    wload(f"X_o_{l}", wo[:, pc * 2:(pc + 1) * 2, :].rearrange("p c j -> p (c j)"), "wo", lo=pc * 2048, n=2048)
                for tb in range(NTB):
                    yb = []
                    for hf in range(2):
                        pb, pk = (psS[hf], ("psS", hf)) if tb % 2 == 0 else (psO[hf], ("psO", hf))
                        for c in range(4):
                            P.op("pe", lambda c=c, pb=pb, tb=tb, hf=hf: nc.tensor.matmul(pb[:, :], lhsT=oT[:, c, tb * 128:(tb + 1) * 128], rhs=wo[:, c, hf * 512:(hf + 1) * 512],
                                                                                         start=(c == 0), stop=(c == 3)), reads=[("oT", tb // 4), "wo"], pwrites=[pk])
                        yb.append((pb[:, :], pk))
                    ln_block(tb, yb, (xres[0], "xres0"), (xres[1], "xres1"))
                barrier()
            if stage == 5:
                for tb in range(NTB):
                    P.op("sp", lambda tb=tb: nc.sync.dma_start(out=dbg[tb * 128:(tb + 1) * 128, :], in_=xres[1][tb * 128:(tb + 1) * 128, :]),
                         reads=[("xd", "xres1", tb)], dma=True)
                break

            with ExitStack() as ph:
                hT = sb(ph, "hT", [128, NJ, 1024], BF16)
                wgu = [sb(ph, f"wgu{i}", [128, 8, 256], BF16) for i in range(2)]
                wd = sb(ph, "wd", [128, NJ, 1024], BF16)
                sgb = [sb(ph, f"sgb{i}", [128, 512], F32) for i in range(2)]
                vload(f"LN_ffn_g_{l}", lng[:], "lng")
                vload(f"LN_ffn_b_{l}", lnb[:], "lnb")
                xdst = (outd, "out") if l == DEPTH - 1 else (xres[2], "xres2")
                cg = 0
                for half in range(2):
                    wload(f"F_gu_{l}_0", wgu[0][:, :, :].rearrange("p c j -> p (c j)"), ("wgu", 0), pw=False)
                    for j in range(NJ):
                        i = j % 2
                        if j + 1 < NJ:
                            wload(f"F_gu_{l}_{j + 1}", wgu[(j + 1) % 2][:, :, :].rearrange("p c j -> p (c j)"), ("wgu", (j + 1) % 2), pw=False)
                        if half == 0 and j % 2 == 1:
                            jj = j // 2
                            wload(f"F_d_{l}_{jj}", wd[:, jj * 2:(jj + 1) * 2, :].rearrange("p c j -> p (c j)"), "wd")
                        for t2 in range(2):
                            tt = half * 2 + t2
                            sl = slice(tt * 512, (tt + 1) * 512)
                            pg, pgk = nextP()
                            pu, puk = nextP()
                            for c in range(8):
                                P.op("pe", lambda c=c, pg=pg, i=i, sl=sl: nc.tensor.matmul(pg[:, :], lhsT=wgu[i][:, c, 0:128], rhs=xT[:, c, sl], start=(c == 0), stop=(c == 7)),
                                     reads=[("wgu", i), ("xT", tt)], pwrites=[pgk])
                            for c in range(8):
                                P.op("pe", lambda c=c, pu=pu, i=i, sl=sl: nc.tensor.matmul(pu[:, :], lhsT=wgu[i][:, c, 128:256], rhs=xT[:, c, sl], start=(c == 0), stop=(c == 7)),
                                     reads=[("wgu", i), ("xT", tt)], pwrites=[puk])
                            si = cg % 2
                            cg += 1
                            P.op("act", lambda pg=pg, si=si: nc.scalar.activation(out=sgb[si][:, :], in_=pg[:, :], func=AF.Silu), reads=[pgk], writes=[("sgb", si)])
                            P.op("dve", lambda pu=pu, si=si, j=j, t2=t2: nc.vector.tensor_tensor(out=hT[:, j, t2 * 512:(t2 + 1) * 512], in0=pu[:, :], in1=sgb[si][:, :], op=ALU.mult),
                                 reads=[puk, ("sgb", si)], pwrites=[("hT", t2)])
                    for t8 in range(8):
                        tb = half * 8 + t8
                        yb = []
                        for hf in range(2):
                            pb, pk = (psS[hf], ("psS", hf)) if t8 % 2 == 0 else (psO[hf], ("psO", hf))
                            for j in range(NJ):
                                P.op("pe", lambda j=j, pb=pb, t8=t8, hf=hf: nc.tensor.matmul(pb[:, :], lhsT=hT[:, j, t8 * 128:(t8 + 1) * 128], rhs=wd[:, j, hf * 512:(hf + 1) * 512],
                                                                                             start=(j == 0), stop=(j == NJ - 1)), reads=[("hT", t8 // 4), "wd"], pwrites=[pk])
                            yb.append((pb[:, :], pk))
                        ln_block(tb, yb, (xres[1], "xres1"), xdst)
                barrier()
            if stage == 6:
                for tb in range(NTB):
                    P.op("sp", lambda tb=tb: nc.sync.dma_start(out=dbg[tb * 128:(tb + 1) * 128, :], in_=xres[2][tb * 128:(tb + 1) * 128, :]),
                         reads=[("xd", "xres2", tb)], dma=True)
                break

        P.flush()
        for k, v in enumerate(P.dcnt):
            if v:
                nc.sync.wait_ge(dsems[k], v)
        print("ops", len(P.ops), "counts", P.cnt, "ndma", P.n_dma, flush=True)
    return nc


_NC_CACHE = {}


def kernel(**inputs):
    wp = pack_weights(inputs)
    if "nc" not in _NC_CACHE:
        _NC_CACHE["nc"] = build()
    nc = _NC_CACHE["nc"]
    x = np.asarray(inputs["x"], dtype=np.float32)
    mem = np.asarray(inputs["mem"], dtype=np.float32)
    pos = np.asarray(inputs["positions"]).astype(np.int32)
    in_maps = []
    for b in range(8):
        in_maps.append({"xin": np.ascontiguousarray(x[b]), "mem": np.ascontiguousarray(mem[b]),
                        "pos": np.ascontiguousarray(pos[b][None, :]), "wpack": wp})
    res = run_bass_kernel_spmd(nc, in_maps, core_ids=list(range(8)))
    out = np.stack([np.asarray(r["out"], dtype=np.float32) for r in res.results], axis=0)
    return out
```

```python
import math
from contextlib import ExitStack

import numpy as np
import concourse.bass as bass
import concourse.mybir as mybir
from concourse.bass_utils import run_bass_kernel_spmd

F32 = mybir.dt.float32
BF16 = mybir.dt.bfloat16
I32 = mybir.dt.int32
AF = mybir.ActivationFunctionType
ALU = mybir.AluOpType

T = 2048
D = 1024
NTB = 16
NTT = 4
DEPTH = 2
ALPHA = float((2 * DEPTH) ** 0.25)
IN_COLS = 6824
FFN_H = 2816
NJ = 22
LN_EPS = 1e-5
RMS_EPS = 1e-6
NEG = -30000.0

SAME_ENG_SYNC = True
N_DMA_SEMS = 24


class Prog:
    def __init__(self, nc):
        self.nc = nc
        self.ops = []
        self.res = {}
        self.n_dma = 0
        self.slot_last = {}
        self.phase_key = ("__phase__",)
        self.last_on_eng = {}

    def _st(self, k):
        st = self.res.get(k)
        if st is None:
            st = {"w": [], "r": [], "pw": [], "pr": []}
            self.res[k] = st
        return st

    def op(self, eng, fn, reads=(), writes=(), pwrites=(), dma=False):
        i = len(self.ops)
        deps = set()
        deps.update(self._st(self.phase_key)["w"])
        for r in reads:
            st = self._st(r)
            deps.update(st["w"])
            st["r"].append(i)
        for w in writes:
            st = self._st(w)
            deps.update(st["w"])
            deps.update(st["r"])
            st["w"] = [i]
            st["r"] = []
            st["pw"] = []
            st["pr"] = []
        for w in pwrites:
            st = self._st(w)
            if st["r"]:
                st["pw"], st["pr"] = st["w"], st["r"]
                st["w"], st["r"] = [], []
            deps.update(st["pw"])
            deps.update(st["pr"])
            st["w"].append(i)
        slot = None
        if dma:
            slot = self.n_dma % N_DMA_SEMS
            self.n_dma += 1
            if slot in self.slot_last:
                deps.add(self.slot_last[slot])
            self.slot_last[slot] = i
        deps.discard(i)
        if not dma:
            self.last_on_eng[eng] = i
        self.ops.append({"eng": eng, "fn": fn, "deps": deps, "dma": dma, "slot": slot})
        return i

    def barrier(self, fn):
        i = len(self.ops)
        st = self._st(self.phase_key)
        deps = set(st["w"]) | set(self.last_on_eng.values()) | set(self.slot_last.values())
        st["w"] = [i]
        st["r"] = []
        self.ops.append({"eng": "pool", "fn": fn, "deps": deps, "dma": False, "slot": None, "bar": True})

    def setup(self, sems, dma_sems):
        nc = self.nc
        self.sems = sems
        self.dma_sems = dma_sems
        self.engobj = {"pe": nc.tensor, "act": nc.scalar, "dve": nc.vector, "pool": nc.gpsimd, "sp": nc.sync}
        self.cnt = {e: 0 for e in self.engobj}
        self.dcnt = [0] * len(dma_sems)
        self.sig = []
        self.seen = {e: {} for e in self.engobj}
        self.start = 0
        self.last_bar = -1

    def flush(self):
        ops = self.ops
        start = self.start
        engobj = self.engobj

        def skip(d, e):
            if d < start and d != self.last_bar:
                return True
            od = ops[d]
            return (not od["dma"]) and od["eng"] == e and (e == "pe" or not SAME_ENG_SYNC)

        has_dep = {}
        for j in range(start, len(ops)):
            o = ops[j]
            for d in o["deps"]:
                if not skip(d, o["eng"]):
                    has_dep[d] = True
        self.sig.extend([None] * (len(ops) - len(self.sig)))
        sig = self.sig
        for j in range(start, len(ops)):
            o = ops[j]
            e = o["eng"]
            eo = engobj[e]
            tgt = {}
            for d in o["deps"]:
                if skip(d, e):
                    continue
                s = sig[d]
                assert s is not None, (j, d)
                if tgt.get(s[0], (None, 0))[1] < s[2]:
                    tgt[s[0]] = (s[1], s[2])
            for key, (sem, val) in tgt.items():
                if self.seen[e].get(key, 0) >= val:
                    continue
                eo.wait_ge(sem, val)
                self.seen[e][key] = val
            ins = o["fn"]()
            if o["dma"]:
                k = o["slot"]
                self.dcnt[k] += 16
                ins.then_inc(self.dma_sems[k], 16)
                sig[j] = (("d", k), self.dma_sems[k], self.dcnt[k])
            elif has_dep.get(j) or o.get("bar"):
                self.cnt[e] += 1
                ins.then_inc(self.sems[e], 1)
                sig[j] = (("e", e), self.sems[e], self.cnt[e])
            o["fn"] = None
            if o.get("bar"):
                self.last_bar = j
        self.start = len(ops)


def fm(W):
    K, n = W.shape
    return np.ascontiguousarray(W.reshape(K // 128, 128, n).transpose(1, 0, 2)).reshape(128, -1)


def _const_block():
    c = np.zeros((128, 385), np.float32)
    c[:, 0:128] = np.eye(128, dtype=np.float32)
    for k in range(64):
        c[k, 128 + 64 + k] = 1.0
    kk = np.arange(128)[:, None]
    qq = np.arange(128)[None, :]
    c[:, 256:384] = (kk <= qq).astype(np.float32)
    half = 16
    inv = (np.float32(10000.0) ** (-(np.arange(half, dtype=np.float32)) / np.float32(half))).astype(np.float32)
    c[64:96, 384] = np.concatenate([inv, inv])
    return c


def _a_bias_idx():
    k = np.arange(128)[:, None]
    col = np.arange(640)[None, :]
    i = col // 128
    qp = col % 128
    rel = i * 128 + qp - k
    ck = k // 64
    cq = 2 * i + qp // 64
    valid = (ck <= cq) & (ck >= cq - 8)
    idx = np.clip(rel, -63, 256) + 63
    return idx, valid


def specs():
    S = []

    def add(name, n, f):
        S.append((name, n, f))

    add("const", 385, lambda I: _const_block())
    idx, valid = _a_bias_idx()
    for l in range(DEPTH):
        def W(I, l=l):
            return I["w_in"][l]
        for p in range(4):
            cols = np.concatenate([p * 128 + np.arange(128), 512 + p * 128 + np.arange(128)])
            add(f"A_qkp_{l}_{p}", 8 * 256, lambda I, cols=cols, W=W: fm(W(I)[:, cols]))
        add(f"A_v_{l}", 8 * 512, lambda I, W=W: fm(W(I)[:, 1024:1536]))
        for h in range(8):
            add(f"A_bias_{l}_{h}", 640,
                lambda I, l=l, h=h: np.where(valid, I["a_rel_bias"][l, h][idx], np.float32(NEG)).astype(np.float32))
        add(f"B_cq_{l}", 8 * 384, lambda I, W=W: fm(W(I)[:, 1536:1920]))
        add(f"B_ckv_{l}", 8 * 256, lambda I, W=W: fm(W(I)[:, 1920:2176]))
        krc = np.concatenate([2176 + np.arange(32), 2176 + np.arange(32), 2176 + np.arange(32), 2176 + 16 + np.arange(16), 2176 + np.arange(16)])
        add(f"B_kr_{l}", 8 * 128, lambda I, W=W, krc=krc: fm(W(I)[:, krc]))
        for h in range(8):
            b = h * 96
            cq = np.concatenate([b + np.arange(64), b + 64 + np.arange(32), b + 64 + 16 + np.arange(16), b + 64 + np.arange(16)])
            add(f"B_uq_{l}_{h}", 3 * 128, lambda I, l=l, cq=cq: fm(I["b_w_uq"][l][:, cq]))
            b2 = h * 128
            ck = b2 + np.arange(64)
            add(f"B_uk_{l}_{h}", 2 * 64, lambda I, l=l, ck=ck: fm(I["b_w_ukv"][l][:, ck]))
        cv = np.concatenate([h * 128 + 64 + np.arange(64) for h in range(8)])
        add(f"B_uv_{l}", 2 * 512, lambda I, l=l, cv=cv: fm(I["b_w_ukv"][l][:, cv]))
        add(f"B_gq_{l}", 3, lambda I, l=l: np.ascontiguousarray(I["b_q_norm"][l].reshape(3, 128).T))
        add(f"B_gkv_{l}", 2, lambda I, l=l: np.ascontiguousarray(I["b_kv_norm"][l].reshape(2, 128).T))
        for h in range(8):
            cols = np.concatenate([2208 + np.arange(h * 64, h * 64 + 64), 2208 + 512 + np.arange(h * 64, h * 64 + 64)])
            add(f"C_qk_{l}_{h}", 8 * 128, lambda I, cols=cols, W=W: fm(W(I)[:, cols]))
        add(f"C_v_{l}", 8 * 512, lambda I, W=W: fm(W(I)[:, 2208 + 1024: 2208 + 1536]))
        add(f"C_f_{l}", 64, lambda I, W=W: fm(W(I)[:, 3744:3752]))

        def bf_blk(I, l=l):
            o = np.zeros((128, 1), np.float32)
            o[0:8, 0] = I["b_forget"][l]
            return o
        add(f"C_bf_{l}", 1, bf_blk)
        for d in range(8):
            for n in range(3):
                add(f"M_{l}_{d}_{n}", 1536,
                    lambda I, l=l, d=d, n=n, W=W: np.concatenate(
                        [fm(W(I)[:, 3752 + n * 1024 + d * 128: 3752 + n * 1024 + d * 128 + 128]),
                         fm(I["w_branch"][l, n][:, d * 128:(d + 1) * 128])], axis=1))
        add(f"M_bg_{l}", 24, lambda I, l=l: np.ascontiguousarray(
            I["b_gate"][l].reshape(3, 8, 128).transpose(2, 0, 1)).reshape(128, 24))
        add(f"M_out_{l}", 8192, lambda I, l=l: fm(I["w_mix_out"][l]))
        for h in range(4):
            add(f"X_q_{l}_{h}", 1024, lambda I, l=l, h=h: fm(I["xa_w_q"][l][:, h * 128:(h + 1) * 128]))
            add(f"X_k_{l}_{h}", 1024, lambda I, l=l, h=h: fm(I["xa_w_kv"][l][:, h * 128:(h + 1) * 128]))
        add(f"X_v_{l}", 4096, lambda I, l=l: fm(I["xa_w_kv"][l][:, 512:1024]))
        add(f"X_o_{l}", 4096, lambda I, l=l: fm(I["xa_w_o"][l]))
        for j in range(NJ):
            cols = np.concatenate([j * 128 + np.arange(128), FFN_H + j * 128 + np.arange(128)])
            add(f"F_gu_{l}_{j}", 2048, lambda I, l=l, cols=cols: fm(I["ffn_w_gu"][l][:, cols]))
        for jj in range(11):
            add(f"F_d_{l}_{jj}", 2048, lambda I, l=l, jj=jj: fm(I["ffn_w_down"][l][jj * 256:(jj + 1) * 256, :]))
        for nm in ("mix", "xa", "ffn"):
            add(f"LN_{nm}_g_{l}", 1024, lambda I, l=l, nm=nm: np.broadcast_to(I[f"ln_{nm}_g"][l][None, :], (128, 1024)))
            add(f"LN_{nm}_b_{l}", 1024, lambda I, l=l, nm=nm: np.broadcast_to(I[f"ln_{nm}_b"][l][None, :], (128, 1024)))
    return S


_SPECS = None
_OFF = None
_NW = 0


def layout():
    global _SPECS, _OFF, _NW
    if _SPECS is None:
        _SPECS = specs()
        _OFF = {}
        o = 0
        for name, n, f in _SPECS:
            _OFF[name] = (o, n)
            o += n
        _NW = o
    return _SPECS, _OFF, _NW


def pack_weights(inputs):
    S, OFF, NW = layout()
    I = {k: np.asarray(v) for k, v in inputs.items()}
    wp = np.empty((128, NW), np.float32)
    for name, n, f in S:
        o = OFF[name][0]
        wp[:, o:o + n] = f(I)
    return wp


def build(stage=None, dbg_shape=None):
    S, OFF, NW = layout()
    nc = bass.Bass("TRN2", target_bir_lowering=False)
    xin = nc.dram_tensor("xin", [T, D], F32, kind="ExternalInput").ap()
    memin = nc.dram_tensor("mem", [256, D], F32, kind="ExternalInput").ap()
    posin = nc.dram_tensor("pos", [1, T], I32, kind="ExternalInput").ap()
    wpk = nc.dram_tensor("wpack", [128, NW], F32, kind="ExternalInput").ap()
    outd = nc.dram_tensor("out", [T, D], F32, kind="ExternalOutput").ap()
    xres = [nc.dram_tensor(f"xres{i}", [T, D], F32, kind="Internal").ap() for i in range(3)]
    dbg = None
    if dbg_shape is not None:
        dbg = nc.dram_tensor("dbg", list(dbg_shape), F32, kind="ExternalOutput").ap()

    es = ExitStack()
    with es:
        P = Prog(nc)
        sems = {e: es.enter_context(nc.semaphore("s_" + e)) for e in ["pe", "act", "dve", "pool", "sp"]}
        dsems = [es.enter_context(nc.semaphore("d%d" % i)) for i in range(N_DMA_SEMS)]
        P.setup(sems, dsems)

        uid = [0]

        def sb(st, name, shape, dt):
            uid[0] += 1
            return st.enter_context(nc.sbuf_tensor(f"{name}_u{uid[0]}", shape, dt))

        xT = sb(es, "xT", [128, 8, T], BF16)
        stg = [sb(es, f"stg{i}", [128, 2048], F32) for i in range(2)]
        ident = sb(es, "ident", [128, 128], BF16)
        shiftI = sb(es, "shiftI", [64, 128], BF16)
        cmask = sb(es, "cmask", [128, 128], BF16)
        invf = sb(es, "invf", [96, 1], F32)
        ones_f = sb(es, "ones_f", [128, 128], F32)
        ones_b = sb(es, "ones_b", [128, 128], BF16)
        memT = sb(es, "memT", [128, 8, 256], BF16)
        cosT = sb(es, "cosT", [96, T], F32)
        sinT = sb(es, "sinT", [96, T], F32)
        LN = {}

        def alloc_ln(st_, nbuf, params=True):
            LN["n"] = nbuf
            LN["xbuf"] = [sb(st_, f"xbuf{i}", [128, 1024], F32) for i in range(nbuf)]
            LN["xb16"] = [sb(st_, f"xb16{i}", [128, 1024], BF16) for i in range(nbuf)]
            LN["lnst"] = [sb(st_, f"lnst{i}", [128, 24], F32) for i in range(nbuf)]
            if params:
                LN["lng"] = sb(st_, "lng", [128, 1024], F32)
                LN["lnb"] = sb(st_, "lnb", [128, 1024], F32)
        dummy = sb(es, "dummy", [128, 8], F32)
        psS = [es.enter_context(nc.psum_tensor(f"psS{i}", [128, 512], F32)) for i in range(3)]
        psO = [es.enter_context(nc.psum_tensor(f"psO{i}", [128, 512], F32)) for i in range(2)]
        psP = [es.enter_context(nc.psum_tensor(f"psP{i}", [128, 512], F32)) for i in range(2)]
        psT = es.enter_context(nc.psum_tensor("psT", [128, 1024], BF16))

        ctr = {"stg": 0, "p": 0, "s": 0, "o": 0, "xb": 0, "pt": 0}
        BIGP_ = [(psP[0], ("psP", 0)), (psP[1], ("psP", 1)), (psS[0], ("psS", 0)), (psS[1], ("psS", 1)), (psS[2], ("psS", 2)),
                 (psO[0], ("psO", 0)), (psO[1], ("psO", 1))]

        def nextP(big=False):
            if big:
                i = ctr.get("pb", 0) % 7
                ctr["pb"] = ctr.get("pb", 0) + 1
                return BIGP_[i]
            i = ctr["p"] % 2
            ctr["p"] += 1
            return psP[i], ("psP", i)

        def barrier():
            P.barrier(lambda: nc.gpsimd.memset(dummy[:, 0:1], 0.0))
            P.flush()

        def wload(name, dst, dkey, lo=0, n=None, cast=None, pw=True, srcview=None):
            off, tot = OFF[name]
            if n is None:
                n = tot
            assert n <= 2048 and lo + n <= tot
            i = ctr["stg"] % 2
            ctr["stg"] += 1
            sk = ("stg", i)
            st = stg[i]
            P.op("sp", lambda: nc.sync.dma_start(out=st[:, 0:n], in_=wpk[:, off + lo: off + lo + n]), writes=[sk], dma=True)
            if cast is None:
                src = st[:, 0:n] if srcview is None else srcview(st[:, 0:n])
                P.op("pool", lambda: nc.gpsimd.tensor_copy(out=dst, in_=src), reads=[sk],
                     **({"pwrites": [dkey]} if pw else {"writes": [dkey]}))
            else:
                cast(st, sk)

        def vload(name, dst, dkey, eng="sp"):
            off, tot = OFF[name]
            P.op("sp", lambda: nc.sync.dma_start(out=dst, in_=wpk[:, off: off + tot]), writes=[dkey], dma=True)

        def transposes_to_xT(src16, skey, tb, evac_eng="dve"):
            for c in range(8):
                P.op("pe", lambda c=c: nc.tensor.transpose(out=psT[:, c * 128:(c + 1) * 128], in_=src16[:, c * 128:(c + 1) * 128],
                                                          identity=ident[:]), reads=[skey, "ident"], pwrites=["psT"])
            if evac_eng == "act":
                P.op("act", lambda: nc.scalar.copy(out=xT[:, :, tb * 128:(tb + 1) * 128],
                                                   in_=psT[:, :].rearrange("p (c t) -> p c t", c=8)),
                     reads=["psT"], pwrites=[("xT", tb // 4)])
            else:
                P.op("dve", lambda: nc.vector.tensor_copy(out=xT[:, :, tb * 128:(tb + 1) * 128],
                                                          in_=psT[:, :].rearrange("p (c t) -> p c t", c=8)),
                     reads=["psT"], pwrites=[("xT", tb // 4)])

        def ln_a(tb, ybanks, xsrc):
            i = ctr["xb"] % LN["n"]
            ctr["xb"] += 1
            xb, xk = LN["xbuf"][i], ("xbuf", i)
            st, stk = LN["lnst"][i], ("lnst", i)
            xsrc_ap, xsrc_n = xsrc
            P.op("sp", lambda: nc.sync.dma_start(out=xb[:], in_=xsrc_ap[tb * 128:(tb + 1) * 128, :]), reads=[("xd", xsrc_n, tb)], writes=[xk], dma=True)
            for hf in range(2):
                yb, yk = ybanks[hf]
                P.op("dve", lambda hf=hf, yb=yb: nc.vector.scalar_tensor_tensor(
                    out=xb[:, hf * 512:(hf + 1) * 512], in0=xb[:, hf * 512:(hf + 1) * 512], scalar=ALPHA, in1=yb,
                    op0=ALU.mult, op1=ALU.add), reads=[yk, xk], writes=[xk])
            for hf in range(2):
                P.op("dve", lambda hf=hf: nc.vector.bn_stats(out=st[:, hf * 6:(hf + 1) * 6], in_=xb[:, hf * 512:(hf + 1) * 512]),
                     reads=[xk], pwrites=[stk])
            P.op("dve", lambda: nc.vector.bn_aggr(out=st[:, 12:14], in_=st[:, 0:12]), reads=[stk], writes=[stk])
            P.op("dve", lambda: nc.vector.tensor_scalar(out=st[:, 14:15], in0=st[:, 13:14], scalar1=LN_EPS, scalar2=None, op0=ALU.add),
                 reads=[stk], writes=[stk])
            return i

        def ln_a2(i):
            st, stk = LN["lnst"][i], ("lnst", i)
            P.op("act", lambda: nc.scalar.activation(out=st[:, 14:15], in_=st[:, 14:15], func=AF.Sqrt), reads=[stk], writes=[stk])
            P.op("dve", lambda: nc.vector.reciprocal(out=st[:, 15:16], in_=st[:, 14:15]), reads=[stk], writes=[stk])
            P.op("dve", lambda: nc.vector.scalar_tensor_tensor(out=st[:, 16:17], in0=st[:, 12:13], scalar=-1.0, in1=st[:, 15:16],
                                                               op0=ALU.mult, op1=ALU.mult), reads=[stk], writes=[stk])

        def ln_early(i):
            xb, xk = LN["xbuf"][i], ("xbuf", i)
            st, stk = LN["lnst"][i], ("lnst", i)
            lng, lnb = LN["lng"], LN["lnb"]
            P.op("act", lambda: nc.scalar.activation(out=xb[:], in_=xb[:], func=AF.Identity, scale=st[:, 15:16], bias=st[:, 16:17]),
                 reads=[stk, xk], writes=[xk])
            P.op("dve", lambda: nc.vector.tensor_tensor(out=xb[:], in0=xb[:], in1=lng[:], op=ALU.mult), reads=[xk, "lng"], writes=[xk])
            P.op("pool", lambda: nc.gpsimd.tensor_tensor(out=xb[:], in0=xb[:], in1=lnb[:], op=ALU.add), reads=[xk, "lnb"], writes=[xk])

        def ln_late(tb, i, xdst):
            xb, xk = LN["xbuf"][i], ("xbuf", i)
            x16, x16k = LN["xb16"][i], ("xb16", i)
            xdst_ap, xdst_n = xdst
            P.op("act", lambda: nc.scalar.copy(out=x16[:], in_=xb[:]), reads=[xk], writes=[x16k])
            P.op("sp", lambda: nc.sync.dma_start(out=xdst_ap[tb * 128:(tb + 1) * 128, :], in_=xb[:]), reads=[xk],
                 writes=[("xd", xdst_n, tb)], dma=True)
            transposes_to_xT(x16, x16k, tb, evac_eng="act")

        def ln_b1(tb, i, xdst):
            xb, xk = LN["xbuf"][i], ("xbuf", i)
            x16, x16k = LN["xb16"][i], ("xb16", i)
            st, stk = LN["lnst"][i], ("lnst", i)
            lng, lnb = LN["lng"], LN["lnb"]
            xdst_ap, xdst_n = xdst
            P.op("act", lambda: nc.scalar.activation(out=xb[:], in_=xb[:], func=AF.Identity, scale=st[:, 15:16], bias=st[:, 16:17]),
                 reads=[stk, xk], writes=[xk])
            P.op("dve", lambda: nc.vector.tensor_tensor(out=xb[:], in0=xb[:], in1=lng[:], op=ALU.mult), reads=[xk, "lng"], writes=[xk])
            P.op("pool", lambda: nc.gpsimd.tensor_tensor(out=xb[:], in0=xb[:], in1=lnb[:], op=ALU.add), reads=[xk, "lnb"], writes=[xk])
            P.op("act", lambda: nc.scalar.copy(out=x16[:], in_=xb[:]), reads=[xk], writes=[x16k])
            P.op("sp", lambda: nc.sync.dma_start(out=xdst_ap[tb * 128:(tb + 1) * 128, :], in_=xb[:]), reads=[xk],
                 writes=[("xd", xdst_n, tb)], dma=True)

        def ln_b2(tb, i):
            x16, x16k = LN["xb16"][i], ("xb16", i)
            transposes_to_xT(x16, x16k, tb, evac_eng="act")

        def pipelined_ln(emit_mm, tbs, xsrc, xdst, LA=2):
            n = len(tbs)
            ybs = {}
            bufi = {}

            def op_(k):
                pr = PAIRS[k % 3]
                yb = []
                for hf in range(2):
                    pb, pk = pr[hf]
                    emit_mm(k, hf, pb, pk)
                    yb.append((pb[:, :], pk))
                ybs[k] = yb
            for k in range(min(LA, n)):
                op_(k)
            for k in range(min(LA, n)):
                bufi[k] = ln_a(tbs[k], ybs.pop(k), xsrc)
            ln_a2(bufi[0])
            ln_early(bufi[0])
            for k in range(n):
                if k + LA < n:
                    op_(k + LA)
                if k + 1 < n:
                    ln_a2(bufi[k + 1])
                    ln_early(bufi[k + 1])
                if k + LA < n:
                    bufi[k + LA] = ln_a(tbs[k + LA], ybs.pop(k + LA), xsrc)
                ln_late(tbs[k], bufi.pop(k), xdst)

        PAIRS = [((psS[0], ("psS", 0)), (psS[1], ("psS", 1))), ((psO[0], ("psO", 0)), (psO[1], ("psO", 1))),
                 ((psP[0], ("psP", 0)), (psP[1], ("psP", 1)))]

        def outproj_ln(emit_mm, xsrc, xdst):
            pipelined_ln(emit_mm, list(range(NTB)), xsrc, xdst)

        def outproj_ln_half(emit_mm, half, xsrc, xdst):
            pipelined_ln(emit_mm, [half * 8 + t for t in range(8)], xsrc, xdst)

        cst = stg[0]
        off_c = OFF["const"][0]
        P.op("sp", lambda: nc.sync.dma_start(out=cst[:, 0:385], in_=wpk[:, off_c: off_c + 385]), writes=[("stg", 0)], dma=True)
        P.op("pool", lambda: nc.gpsimd.tensor_copy(out=ident[:], in_=cst[:, 0:128]), reads=[("stg", 0)], writes=["ident"])
        P.op("pool", lambda: nc.gpsimd.tensor_copy(out=shiftI[:], in_=cst[0:64, 128:256]), reads=[("stg", 0)], writes=["shiftI"])
        P.op("pool", lambda: nc.gpsimd.tensor_copy(out=cmask[:], in_=cst[:, 256:384]), reads=[("stg", 0)], writes=["cmask"])
        P.op("pool", lambda: nc.gpsimd.tensor_copy(out=invf[64:96, :], in_=cst[64:96, 384:385]), reads=[("stg", 0)], writes=["invf"])
        P.op("pool", lambda: nc.gpsimd.memset(ones_f[:], 1.0), writes=["ones_f"])
        P.op("pool", lambda: nc.gpsimd.memset(ones_b[:], 1.0), writes=["ones_b"])
        ctr["stg"] = 1
        with ExitStack() as ph:
            posi_ = sb(ph, "posi", [96, T], I32)
            ang_ = sb(ph, "ang", [96, T], F32)
            kf_ = sb(ph, "kf", [96, T], F32)
            ki_ = sb(ph, "ki", [96, T], I32)
            r2_ = sb(ph, "r2", [96, T], F32)
            posi, ang, kf, ki, r2 = posi_[64:96, :], ang_[64:96, :], kf_[64:96, :], ki_[64:96, :], r2_[64:96, :]
            cosT_, sinT_ = cosT[64:96, :], sinT[64:96, :]
            P.op("sp", lambda: nc.sync.dma_start(out=posi, in_=posin.partition_broadcast(32)), writes=["posi"], dma=True)
            P.op("dve", lambda: nc.vector.tensor_copy(out=ang, in_=posi), reads=["posi"], writes=["ang"])
            P.op("dve", lambda: nc.vector.tensor_scalar(out=ang, in0=ang, scalar1=invf[64:96, 0:1], scalar2=None, op0=ALU.mult),
                 reads=["ang", "invf"], writes=["ang"])

            def reduce_sin(src, skey, dst, dkey, shift):
                P.op("dve", lambda: nc.vector.tensor_scalar(out=r2, in0=src, scalar1=shift, scalar2=None, op0=ALU.add),
                     reads=[skey], writes=["r2"])
                P.op("dve", lambda: nc.vector.tensor_scalar(out=kf, in0=r2, scalar1=1.0 / (2 * math.pi), scalar2=None, op0=ALU.mult),
                     reads=["r2"], writes=["kf"])
                P.op("dve", lambda: nc.vector.tensor_copy(out=ki, in_=kf), reads=["kf"], writes=["ki"])
                P.op("dve", lambda: nc.vector.tensor_copy(out=kf, in_=ki), reads=["ki"], writes=["kf"])
                C1 = 6.28125
                C2 = 2 * math.pi - 6.28125
                P.op("dve", lambda: nc.vector.scalar_tensor_tensor(out=r2, in0=kf, scalar=-C1, in1=r2, op0=ALU.mult, op1=ALU.add),
                     reads=["kf", "r2"], writes=["r2"])
                P.op("dve", lambda: nc.vector.scalar_tensor_tensor(out=r2, in0=kf, scalar=-C2, in1=r2, op0=ALU.mult, op1=ALU.add),
                     reads=["kf", "r2"], writes=["r2"])
                P.op("dve", lambda: nc.vector.tensor_scalar(out=kf, in0=r2, scalar1=0.0, scalar2=2 * math.pi, op0=ALU.is_lt, op1=ALU.mult),
                     reads=["r2"], writes=["kf"])
                P.op("dve", lambda: nc.vector.tensor_tensor(out=r2, in0=r2, in1=kf, op=ALU.add), reads=["kf", "r2"], writes=["r2"])
                P.op("dve", lambda: nc.vector.tensor_scalar(out=kf, in0=r2, scalar1=2 * math.pi, scalar2=-2 * math.pi, op0=ALU.is_ge, op1=ALU.mult),
                     reads=["r2"], writes=["kf"])
                P.op("dve", lambda: nc.vector.tensor_tensor(out=r2, in0=r2, in1=kf, op=ALU.add), reads=["kf", "r2"], writes=["r2"])
                P.op("dve", lambda: nc.vector.tensor_scalar(out=r2, in0=r2, scalar1=0.0, scalar2=2 * math.pi, op0=ALU.max, op1=ALU.min),
                     reads=["r2"], writes=["r2"])
                P.op("act", lambda: nc.scalar.activation(out=dst, in_=r2, func=AF.Sin, scale=-1.0, bias=math.pi),
                     reads=["r2"], writes=[dkey])

            reduce_sin(ang, "ang", sinT_, "sinT", 0.0)
            reduce_sin(ang, "ang", cosT_, "cosT", math.pi / 2)
            barrier()
        su = ExitStack()
        alloc_ln(su, 3, params=False)
        xbuf, xb16 = LN["xbuf"], LN["xb16"]
        for mb in range(2):
            i = ctr["xb"] % 3
            ctr["xb"] += 1
            xb, xk = xbuf[i], ("xbuf", i)
            x16, x16k = xb16[i], ("xb16", i)
            P.op("sp", lambda mb=mb, xb=xb: nc.sync.dma_start(out=xb[:], in_=memin[mb * 128:(mb + 1) * 128, :]), writes=[xk], dma=True)
            P.op("act", lambda xb=xb, x16=x16: nc.scalar.copy(out=x16[:], in_=xb[:]), reads=[xk], writes=[x16k])
            for c in range(8):
                P.op("pe", lambda c=c, x16=x16: nc.tensor.transpose(out=psT[:, c * 128:(c + 1) * 128], in_=x16[:, c * 128:(c + 1) * 128],
                                                                   identity=ident[:]), reads=[x16k, "ident"], pwrites=["psT"])
            P.op("dve", lambda mb=mb: nc.vector.tensor_copy(out=memT[:, :, mb * 128:(mb + 1) * 128],
                                                           in_=psT[:, :].rearrange("p (c t) -> p c t", c=8)),
                 reads=["psT"], pwrites=["memT"])
        for tb in range(NTB):
            i = ctr["xb"] % 3
            ctr["xb"] += 1
            xb, xk = xbuf[i], ("xbuf", i)
            x16, x16k = xb16[i], ("xb16", i)
            P.op("sp", lambda tb=tb, xb=xb: nc.sync.dma_start(out=xb[:], in_=xin[tb * 128:(tb + 1) * 128, :]), writes=[xk], dma=True)
            P.op("act", lambda xb=xb, x16=x16: nc.scalar.copy(out=x16[:], in_=xb[:]), reads=[xk], writes=[x16k])
            transposes_to_xT(x16, x16k, tb)
        barrier()
        su.close()

        def proj_fm(wb, wkey, ncc, col0, M, src, srckeyf, evac, mrow0=0):
            for tt in range(NTT):
                pb, pk = nextP()
                for c in range(ncc):
                    P.op("pe", lambda c=c, tt=tt, pb=pb: nc.tensor.matmul(pb[0:M, :], lhsT=wb[:, c, col0:col0 + M],
                                                                          rhs=src[:, c, tt * 512:(tt + 1) * 512],
                                                                          start=(c == 0), stop=(c == ncc - 1)),
                         reads=[wkey, srckeyf(tt)], pwrites=[pk])
                evac(tt, pb, pk)

        def attention(n, h, qT, qk, kT, kk, V, vk, Kd, kind, biasb=None, bk=None, pTs=None, tmps=None, rc=None, bcs=None, tmpo=None, ybr=None, r0=0):
            steps = []
            for qt in range(NTT):
                if kind == "A":
                    kbs = list(range(max(0, 4 * qt - 4), 4 * qt + 4))
                else:
                    kbs = list(range(0, 4 * qt + 4))
                for ii, kb in enumerate(kbs):
                    if kind == "A":
                        jlo = max(kb, 4 * qt)
                        jhi = min(kb + 4, 4 * qt + 3)
                    else:
                        jlo = max(kb, 4 * qt)
                        jhi = 4 * qt + 3
                    c0 = (jlo - 4 * qt) * 128
                    c1 = (jhi - 4 * qt + 1) * 128
                    steps.append(dict(qt=qt, kb=kb, c0=c0, c1=c1, first=(ii == 0), last=(ii == len(kbs) - 1),
                                      diag=(kb >= 4 * qt), b0=(jlo - kb) * 128))
            nS = len(steps)
            sbank = {}
            obank = {}
            pending = PEND
            for k_ in range(len(pending)):
                pending[k_] = (0, pending[k_][1])

            def emit_S(i):
                s = steps[i]
                bi = ctr["s"] % 3
                ctr["s"] += 1
                sbank[i] = bi
                q0 = s["qt"] * 512
                P.op("pe", lambda s=s, bi=bi: nc.tensor.matmul(psS[bi][:, s["c0"]:s["c1"]], lhsT=kT[r0:r0 + Kd, s["kb"] * 128:(s["kb"] + 1) * 128],
                                                               rhs=qT[r0:r0 + Kd, q0 + s["c0"]: q0 + s["c1"]], start=True, stop=True),
                     reads=[kk, qk], pwrites=[("psS", bi)])

            def emit_exp(i):
                s = steps[i]
                bi = sbank[i]
                pi = ctr["pt"] % len(pTs)
                ctr["pt"] += 1
                s["pi"] = pi
                pT = pTs[pi]
                pk = ("pT", pi)
                c0, c1 = s["c0"], s["c1"]
                if kind == "A":
                    ti = pi % len(tmps)
                    tm = tmps[ti]
                    P.op("dve", lambda: nc.vector.tensor_tensor(out=tm[:, c0:c1], in0=psS[bi][:, c0:c1], in1=biasb[:, s["b0"]: s["b0"] + c1 - c0],
                                                                op=ALU.add), reads=[("psS", bi), bk], writes=[("tmp", ti)])
                    P.op("act", lambda: nc.scalar.activation(out=pT[:, c0:c1], in_=tm[:, c0:c1], func=AF.Exp), reads=[("tmp", ti)], writes=[pk])
                else:
                    P.op("act", lambda: nc.scalar.activation(out=pT[:, c0:c1], in_=psS[bi][:, c0:c1], func=AF.Exp), reads=[("psS", bi)], writes=[pk])
                    if s["diag"]:
                        if kind == "B":
                            P.op("pool", lambda: nc.gpsimd.memset(pT[64:128, c0:c0 + 64], 0.0), reads=[pk], writes=[pk])
                        elif kind == "C":
                            P.op("pool", lambda: nc.gpsimd.tensor_tensor(out=pT[:, c0:c0 + 128], in0=pT[:, c0:c0 + 128], in1=cmask[:],
                                                                         op=ALU.mult), reads=[pk, "cmask"], writes=[pk])

            def emit_PV(i):
                s = steps[i]
                if s["first"]:
                    oi = ctr["o"] % 2
                    ctr["o"] += 1
                    obank[s["qt"]] = oi
                oi = obank[s["qt"]]
                pT = pTs[s["pi"]]
                c0, c1 = s["c0"], s["c1"]
                P.op("pe", lambda: nc.tensor.matmul(psO[oi][0:65, c0:c1], lhsT=V[:, s["kb"], 0:65], rhs=pT[:, c0:c1],
                                                    start=s["first"], stop=s["last"], skip_group_check=True),
                     reads=[vk, ("pT", s["pi"])], pwrites=[("psO", oi)])
                if s["last"]:
                    qt = s["qt"]
                    ok = ("psO", oi)
                    P.op("act", lambda: nc.scalar.activation(out=rc[64:65, :], in_=psO[oi][64:65, :], func=AF.Ln), reads=[ok], writes=["rc"])
                    P.op("act", lambda: nc.scalar.activation(out=rc[64:65, :], in_=rc[64:65, :], func=AF.Exp, scale=-1.0), reads=["rc"], writes=["rc"])

                    def norm():
                        pb, pk = nextP()
                        P.op("pe", lambda: nc.tensor.matmul(pb[:, :], lhsT=ones_f[64:65, 0:128], rhs=rc[64:65, :], start=True, stop=True),
                             reads=["rc", "ones_f"], pwrites=[pk])
                        bi = ctr.get("bc", 0) % len(bcs)
                        ctr["bc"] = ctr.get("bc", 0) + 1
                        bc = bcs[bi]
                        P.op("dve", lambda: nc.vector.tensor_copy(out=bc[0:64, :], in_=pb[0:64, :]), reads=[pk], writes=[("bc", bi)])
                        cc = h // 2
                        if h % 2 == 0:
                            P.op("dve", lambda: nc.vector.tensor_tensor(out=ybr[0:64, cc, qt * 512:(qt + 1) * 512], in0=psO[oi][0:64, :],
                                                                        in1=bc[0:64, :], op=ALU.mult),
                                 reads=[ok, ("bc", bi)], pwrites=[("y", n, qt)])
                        else:
                            P.op("dve", lambda: nc.vector.tensor_tensor(out=tmpo[0:64, :], in0=psO[oi][0:64, :], in1=bc[0:64, :], op=ALU.mult),
                                 reads=[ok, ("bc", bi)], writes=["tmpo"])
                            pb2, pk2 = nextP()
                            P.op("pe", lambda: nc.tensor.matmul(pb2[:, :], lhsT=shiftI[0:64, :], rhs=tmpo[0:64, :], start=True, stop=True),
                                 reads=["tmpo", "shiftI"], pwrites=[pk2])
                            P.op("dve", lambda: nc.vector.tensor_copy(out=ybr[64:128, cc, qt * 512:(qt + 1) * 512], in_=pb2[64:128, :]),
                                 reads=[pk2], pwrites=[("y", n, qt)])
                    pending.append((i + 3, norm))

            LA = 2
            for i in range(min(LA, nS)):
                emit_S(i)
            for i in range(nS):
                while pending and pending[0][0] <= i:
                    pending.pop(0)[1]()
                if i + LA < nS:
                    emit_S(i + LA)
                emit_exp(i)
                emit_PV(i)

        PEND = []

        def flush_pend():
            while PEND:
                PEND.pop(0)[1]()

        for l in range(DEPTH):
            xsrc = (xin, "xin") if l == 0 else (xres[2], "xres2")
            with ExitStack() as mx:
                ybrs = [sb(mx, f"ybr{n}", [128, 4, T], BF16) for n in range(3)]
                def plain_mixer(n, kind):
                    pref = "A" if kind == "A" else "C"
                    with ExitStack() as ph:
                        Kd = 64
                        biasbs = None
                        tmps = None
                        Vall = sb(ph, "Vall", [128, NTB, 8, 65], BF16)
                        P.op("pool", lambda: nc.gpsimd.memset(Vall[:, :, :, 64:65], 1.0), writes=["vall"])
                        if kind == "C":
                            Kd = 70
                            Fp = [sb(ph, f"Fp{i}", [8, T], BF16) for i in range(3)]
                            Fn = [sb(ph, f"Fn{i}", [8, T], BF16) for i in range(3)]
                        with ExitStack() as pp_:
                            wv = sb(pp_, "wv", [128, 8, 512], BF16)
                            for pc in range(2):
                                wload(f"{pref}_v_{l}", wv[:, pc * 4:(pc + 1) * 4, :].rearrange("p c j -> p (c j)"), "wv", lo=pc * 2048, n=2048)
                            for tb in range(NTB):
                                pb, pk = nextP(big=True)
                                for c in range(8):
                                    P.op("pe", lambda c=c, tb=tb, pb=pb: nc.tensor.matmul(pb[:, :], lhsT=xT[:, c, tb * 128:(tb + 1) * 128], rhs=wv[:, c, :],
                                                                                          start=(c == 0), stop=(c == 7)), reads=["wv", ("xT", tb // 4)], pwrites=[pk])
                                if tb % 2 == 0:
                                    P.op("act", lambda tb=tb, pb=pb: nc.scalar.copy(out=Vall[:, tb, :, 0:64], in_=pb[:, :].rearrange("p (h d) -> p h d", h=8)),
                                         reads=[pk], pwrites=["vall"])
                                else:
                                    P.op("dve", lambda tb=tb, pb=pb: nc.vector.tensor_copy(out=Vall[:, tb, :, 0:64], in_=pb[:, :].rearrange("p (h d) -> p h d", h=8)),
                                         reads=[pk], pwrites=["vall"])
                            if kind == "C":
                                wf = sb(pp_, "wf", [128, 8, 8], BF16)
                                bfg = sb(pp_, "bfg", [8, 1], F32)
                                lf = sb(pp_, "lf", [8, T], F32)
                                Ff = sb(pp_, "Ff", [8, T], F32)
                                wload(f"C_f_{l}", wf[:, :, :].rearrange("p c j -> p (c j)"), "wf", pw=False)
                                offb = OFF[f"C_bf_{l}"][0]
                                P.op("sp", lambda: nc.sync.dma_start(out=bfg[:], in_=wpk[0:8, offb:offb + 1], allow_slow_non_contiguous=True), writes=["bfg"], dma=True)
                                P.op("dve", lambda: nc.vector.tensor_scalar(out=bfg[:], in0=bfg[:], scalar1=-1.0, scalar2=None, op0=ALU.mult),
                                     reads=["bfg"], writes=["bfg"])

                                def evac_f(tt, pb, pk):
                                    P.op("act", lambda: nc.scalar.activation(out=lf[:, tt * 512:(tt + 1) * 512], in_=pb[0:8, :], func=AF.Exp,
                                                                             scale=-1.0, bias=bfg[:, 0:1]), reads=[pk, "bfg"], pwrites=["lf"])
                                proj_fm(wf, "wf", 8, 0, 8, xT, lambda tt: ("xT", tt), evac_f)
                                P.op("act", lambda: nc.scalar.activation(out=lf[:], in_=lf[:], func=AF.Ln, bias=1.0, scale=1.0), reads=["lf"], writes=["lf"])
                                P.op("dve", lambda: nc.vector.tensor_tensor_scan(out=Ff[:], data0=ones_f[0:8, 0:1].to_broadcast([8, T]), data1=lf[:],
                                                                                 initial=0.0, op0=ALU.mult, op1=ALU.subtract),
                                     reads=["lf", "ones_f"], writes=["Ff"])
                                for i3 in range(3):
                                    P.op("dve", lambda i3=i3: nc.vector.tensor_copy(out=Fp[i3][:], in_=Ff[:]), reads=["Ff"], writes=[("Fp", i3)])
                                    P.op("dve", lambda i3=i3: nc.vector.tensor_scalar(out=Fn[i3][:], in0=Fp[i3][:], scalar1=-1.0, scalar2=None, op0=ALU.mult),
                                         reads=[("Fp", i3)], writes=[("Fn", i3)])
                                    if i3 < 2:
                                        P.op("dve", lambda i3=i3: nc.vector.tensor_tensor(out=Ff[:], in0=Ff[:], in1=Fp[i3][:], op=ALU.subtract),
                                             reads=["Ff", ("Fp", i3)], writes=["Ff"])
                            barrier()
                        ncol = 256 if kind == "A" else 128
                        wbs = [sb(ph, f"wb{i}", [128, 8, ncol], BF16) for i in range(2)]
                        qTs = [sb(ph, f"qT{i}", [128, T], BF16) for i in range(2)]
                        kTs = [sb(ph, f"kT{i}", [128, T], BF16) for i in range(2)]
                        pTs = [sb(ph, f"pT{i}", [128, 512], BF16) for i in range(4)]
                        rc = sb(ph, "rc", [128, 512], F32)
                        bcs = [sb(ph, f"bc{i}", [64, 512], F32) for i in range(2)]
                        tmpo = sb(ph, "tmpo", [64, 512], BF16)
                        if kind == "A":
                            biasbs = [sb(ph, f"biasb{i}", [128, 640], F32) for i in range(2)]
                            tmps = [sb(ph, f"tmp{i}", [128, 512], F32) for i in range(2)]

                            def load_pair(p):
                                wload(f"A_qkp_{l}_{p}", wbs[p % 2][:, :, :].rearrange("p c j -> p (c j)"), ("wb", p % 2), pw=False)

                            def load_bias(h):
                                vload(f"A_bias_{l}_{h}", biasbs[h % 2][:], ("biasb", h % 2))
                            load_pair(0)
                            load_bias(0)
                            for p in range(4):
                                i = p % 2
                                if p + 1 < 4:
                                    load_pair(p + 1)
                                wb, wk = wbs[i], ("wb", i)
                                qT, qk = qTs[i], ("q", i)
                                kT, kk = kTs[i], ("k", i)

                                def evq(tt, pb, pk, qT=qT, qk=qk):
                                    P.op("dve", lambda: nc.vector.tensor_scalar(out=qT[:, tt * 512:(tt + 1) * 512], in0=pb[:, :], scalar1=0.125,
                                                                                scalar2=None, op0=ALU.mult), reads=[pk], pwrites=[qk])

                                def evk(tt, pb, pk, kT=kT, kk=kk):
                                    P.op("act", lambda: nc.scalar.copy(out=kT[:, tt * 512:(tt + 1) * 512], in_=pb[:, :]), reads=[pk], pwrites=[kk])
                                proj_fm(wb, wk, 8, 0, 128, xT, lambda tt: ("xT", tt), evq)
                                proj_fm(wb, wk, 8, 128, 128, xT, lambda tt: ("xT", tt), evk)
                                for hh in range(2):
                                    h = 2 * p + hh
                                    if h + 1 < 8:
                                        load_bias(h + 1)
                                    attention(n, h, qT, qk, kT, kk, Vall[:, :, h, :], "vall", 64, "A", biasb=biasbs[h % 2], bk=("biasb", h % 2),
                                              pTs=pTs, tmps=tmps, rc=rc, bcs=bcs, tmpo=tmpo, ybr=ybrs[n], r0=hh * 64)
                        else:
                            for i2 in range(2):
                                P.op("pool", lambda i2=i2: nc.gpsimd.memset(qTs[i2][64:70, :], 1.0), writes=[("q", i2)])
                                P.op("pool", lambda i2=i2: nc.gpsimd.memset(kTs[i2][64:70, :], 1.0), writes=[("k", i2)])

                            def load_head(h):
                                wload(f"C_qk_{l}_{h}", wbs[h % 2][:, :, :].rearrange("p c j -> p (c j)"), ("wb", h % 2), pw=False)
                            load_head(0)
                            for h in range(8):
                                i = h % 2
                                if h + 1 < 8:
                                    load_head(h + 1)
                                wb, wk = wbs[i], ("wb", i)
                                qT, qk = qTs[i], ("q", i)
                                kT, kk = kTs[i], ("k", i)

                                def evq(tt, pb, pk, qT=qT, qk=qk):
                                    P.op("dve", lambda: nc.vector.tensor_scalar(out=qT[0:64, tt * 512:(tt + 1) * 512], in0=pb[0:64, :], scalar1=0.125,
                                                                                scalar2=None, op0=ALU.mult), reads=[pk], pwrites=[qk])

                                def evk(tt, pb, pk, kT=kT, kk=kk):
                                    P.op("act", lambda: nc.scalar.copy(out=kT[0:64, tt * 512:(tt + 1) * 512], in_=pb[0:64, :]), reads=[pk], pwrites=[kk])
                                proj_fm(wb, wk, 8, 0, 64, xT, lambda tt: ("xT", tt), evq)
                                proj_fm(wb, wk, 8, 64, 64, xT, lambda tt: ("xT", tt), evk)
                                for i3 in range(3):
                                    P.op("sp", lambda i3=i3, qT=qT, h=h: nc.sync.dma_start(out=qT[64 + i3:65 + i3, :], in_=Fp[i3][h:h + 1, :]),
                                         reads=[("Fp", i3)], pwrites=[qk], dma=True)
                                    P.op("sp", lambda i3=i3, kT=kT, h=h: nc.sync.dma_start(out=kT[67 + i3:68 + i3, :], in_=Fn[i3][h:h + 1, :]),
                                         reads=[("Fn", i3)], pwrites=[kk], dma=True)
                                attention(n, h, qT, qk, kT, kk, Vall[:, :, h, :], "vall", Kd, "C", pTs=pTs, rc=rc, bcs=bcs, tmpo=tmpo, ybr=ybrs[n])
                        flush_pend()
                        barrier()

                def mla_mixer(n):
                    with ExitStack() as ph:
                        cqT = sb(ph, "cqT", [128, 3, T], BF16)
                        ckvT = sb(ph, "ckvT", [128, 2, T], BF16)
                        Vall = sb(ph, "Vall", [128, NTB, 8, 65], BF16)
                        kTs = [sb(ph, f"kT{i}", [128, T], BF16) for i in range(2)]
                        gq = sb(ph, "gq", [128, 3], F32)
                        gkv = sb(ph, "gkv", [128, 2], F32)
                        vload(f"B_gq_{l}", gq[:], "gq")
                        vload(f"B_gkv_{l}", gkv[:], "gkv")
                        P.op("pool", lambda: nc.gpsimd.memset(Vall[:, :, :, 64:65], 1.0), writes=["vall"])
                        with ExitStack() as pp_:
                            wbig = sb(pp_, "wbig", [128, 8, 384], BF16)
                            sq = sb(pp_, "sq", [128, 3, 512], F32)
                            rs = sb(pp_, "rs", [128, 512], F32)
                            wkr = sb(pp_, "wkr", [128, 8, 128], BF16)
                            wuv = sb(pp_, "wuv", [128, 2, 512], BF16)
                            rt = [sb(pp_, f"rt{i}", [96, 512], F32) for i in range(2)]

                            SETS = [[(psS[0], ("psS", 0)), (psS[1], ("psS", 1)), (psS[2], ("psS", 2))],
                                    [(psO[0], ("psO", 0)), (psO[1], ("psO", 1)), (psP[0], ("psP", 0))]]
                            rs2 = [rs, sb(pp_, "rs2", [128, 512], F32)]

                            def latent(name, ncol, nch, dstT, dkey, dim, scale):
                                for half in range(2):
                                    wload(name, wbig[:, half * 4:(half + 1) * 4, 0:ncol], "wbig", lo=half * 4 * ncol, n=4 * ncol,
                                          srcview=lambda a: a.rearrange("p (c j) -> p c j", c=4))

                                def proj_tt(tt):
                                    sl = slice(tt * 512, (tt + 1) * 512)
                                    bs = SETS[tt % 2]
                                    r_ = rs2[tt % 2]
                                    rk = ("rs", tt % 2)
                                    for j in range(nch):
                                        pbj, pkj = bs[j]
                                        for c in range(8):
                                            P.op("pe", lambda c=c, j=j, pbj=pbj: nc.tensor.matmul(
                                                pbj[:, :], lhsT=wbig[:, c, j * 128:(j + 1) * 128], rhs=xT[:, c, sl],
                                                start=(c == 0), stop=(c == 7)), reads=["wbig", ("xT", tt)], pwrites=[pkj])
                                        P.op("act", lambda j=j, pbj=pbj: nc.scalar.activation(out=sq[:, j, :], in_=pbj[:, :], func=AF.Square),
                                             reads=[pkj], pwrites=["sq"])
                                    pb, pk = psP[1], ("psP", 1)
                                    for j in range(nch):
                                        P.op("pe", lambda j=j: nc.tensor.matmul(pb[:, :], lhsT=ones_f[:, :], rhs=sq[:, j, :],
                                                                                start=(j == 0), stop=(j == nch - 1)),
                                             reads=["sq", "ones_f"], pwrites=[pk])
                                    P.op("dve", lambda: nc.vector.tensor_scalar(out=r_[:, :], in0=pb[:, :], scalar1=1.0 / dim, scalar2=RMS_EPS,
                                                                                op0=ALU.mult, op1=ALU.add), reads=[pk], writes=[rk])
                                    P.op("act", lambda: nc.scalar.activation(out=r_[:, :], in_=r_[:, :], func=AF.Ln), reads=[rk], writes=[rk])
                                    P.op("act", lambda: nc.scalar.activation(out=r_[:, :], in_=r_[:, :], func=AF.Exp, scale=-0.5), reads=[rk], writes=[rk])
                                    if scale != 1.0:
                                        P.op("dve", lambda: nc.vector.tensor_scalar(out=r_[:, :], in0=r_[:, :], scalar1=scale, scalar2=None, op0=ALU.mult),
                                             reads=[rk], writes=[rk])

                                def norm_tt(tt):
                                    sl = slice(tt * 512, (tt + 1) * 512)
                                    bs = SETS[tt % 2]
                                    r_ = rs2[tt % 2]
                                    rk = ("rs", tt % 2)
                                    for j in range(nch):
                                        pbj, pkj = bs[j]
                                        P.op("dve", lambda j=j, pbj=pbj: nc.vector.tensor_tensor(out=dstT[:, j, sl], in0=pbj[:, :], in1=r_[:, :], op=ALU.mult),
                                             reads=[pkj, rk], pwrites=[dkey])
                                proj_tt(0)
                                for tt in range(NTT):
                                    if tt + 1 < NTT:
                                        proj_tt(tt + 1)
                                    norm_tt(tt)

                            latent(f"B_cq_{l}", 384, 3, cqT, "cqT", 384.0, 96.0 ** -0.5)
                            latent(f"B_ckv_{l}", 256, 2, ckvT, "ckvT", 256.0, 1.0)

                            def cast_kr(st, sk):
                                v = st[:, 0:1024].rearrange("p (c j) -> p c j", c=8)
                                P.op("pool", lambda: nc.gpsimd.tensor_copy(out=wkr[:, :, 0:96], in_=v[:, :, 0:96]), reads=[sk], pwrites=["wkr"])
                                P.op("pool", lambda: nc.gpsimd.tensor_scalar(out=wkr[:, :, 96:112], in0=v[:, :, 96:112], scalar1=-1.0, scalar2=None, op0=ALU.mult),
                                     reads=[sk], pwrites=["wkr"])
                                P.op("pool", lambda: nc.gpsimd.tensor_copy(out=wkr[:, :, 112:128], in_=v[:, :, 112:128]), reads=[sk], pwrites=["wkr"])
                            wload(f"B_kr_{l}", None, None, cast=cast_kr)
                            for tt in range(NTT):
                                sl = slice(tt * 512, (tt + 1) * 512)
                                pb, pk = nextP()
                                pb2, pk2 = nextP()
                                for c in range(8):
                                    P.op("pe", lambda c=c, sl=sl, pb=pb: nc.tensor.matmul(pb[0:96, :], lhsT=wkr[:, c, 0:96], rhs=xT[:, c, sl],
                                                                                          start=(c == 0), stop=(c == 7)), reads=["wkr", ("xT", tt)], pwrites=[pk])
                                for c in range(8):
                                    P.op("pe", lambda c=c, sl=sl, pb2=pb2: nc.tensor.matmul(pb2[0:96, :], lhsT=wkr[:, c, 32:128], rhs=xT[:, c, sl],
                                                                                            start=(c == 0), stop=(c == 7)), reads=["wkr", ("xT", tt)], pwrites=[pk2])
                                P.op("dve", lambda sl=sl, pb=pb: nc.vector.tensor_tensor(out=rt[0][64:96, :], in0=pb[64:96, :], in1=cosT[64:96, sl], op=ALU.mult),
                                     reads=[pk, "cosT"], writes=[("rt", 0)])
                                P.op("dve", lambda sl=sl, pb2=pb2: nc.vector.tensor_tensor(out=rt[1][64:96, :], in0=pb2[64:96, :], in1=sinT[64:96, sl], op=ALU.mult),
                                     reads=[pk2, "sinT"], writes=[("rt", 1)])
                                for i2 in range(2):
                                    P.op("dve", lambda sl=sl, i2=i2: nc.vector.tensor_tensor(out=kTs[i2][64:96, sl], in0=rt[0][64:96, :], in1=rt[1][64:96, :], op=ALU.add),
                                         reads=[("rt", 0), ("rt", 1)], pwrites=[("k", i2)])

                            def cast_uv(st, sk):
                                for j in range(2):
                                    P.op("pool", lambda j=j: nc.gpsimd.tensor_scalar(out=wuv[:, j, :], in0=st[:, j * 512:(j + 1) * 512], scalar1=gkv[:, j:j + 1],
                                                                                     scalar2=None, op0=ALU.mult), reads=[sk, "gkv"], pwrites=["wuv"])
                            wload(f"B_uv_{l}", None, None, cast=cast_uv)
                            for tb in range(NTB):
                                pb, pk = nextP()
                                for j in range(2):
                                    P.op("pe", lambda j=j, tb=tb, pb=pb: nc.tensor.matmul(pb[:, :], lhsT=ckvT[:, j, tb * 128:(tb + 1) * 128], rhs=wuv[:, j, :],
                                                                                          start=(j == 0), stop=(j == 1)), reads=["ckvT", "wuv"], pwrites=[pk])
                                P.op("act", lambda tb=tb, pb=pb: nc.scalar.copy(out=Vall[:, tb, :, 0:64], in_=pb[:, :].rearrange("p (h d) -> p h d", h=8)),
                                     reads=[pk], pwrites=["vall"])
                            barrier()
                        qTs = [sb(ph, f"qT{i}", [128, T], BF16) for i in range(2)]
                        pTs = [sb(ph, f"pT{i}", [128, 512], BF16) for i in range(3)]
                        rc = sb(ph, "rc", [128, 512], F32)
                        bcs = [sb(ph, f"bc{i}", [64, 512], F32) for i in range(1)]
                        tmpo = sb(ph, "tmpo", [64, 512], BF16)
                        wuq = [sb(ph, f"wuq{i}", [128, 3, 128], BF16) for i in range(2)]
                        wuk = [sb(ph, f"wuk{i}", [128, 2, 64], BF16) for i in range(2)]
                        rt = [sb(ph, f"rtb{i}", [96, 512], F32) for i in range(2)]

                        def load_head(h):
                            i = h % 2

                            def cast_uq(st, sk, i=i):
                                v = st[:, 0:384].rearrange("p (c j) -> p c j", c=3)
                                for j in range(3):
                                    P.op("pool", lambda j=j: nc.gpsimd.tensor_scalar(out=wuq[i][:, j, :], in0=v[:, j, :], scalar1=gq[:, j:j + 1],
                                                                                     scalar2=None, op0=ALU.mult), reads=[sk, "gq"], pwrites=[("wuq", i)])
                                P.op("pool", lambda: nc.gpsimd.tensor_scalar(out=wuq[i][:, :, 96:112], in0=wuq[i][:, :, 96:112], scalar1=-1.0, scalar2=None,
                                                                             op0=ALU.mult), reads=[("wuq", i)], writes=[("wuq", i)])

                            def cast_uk(st, sk, i=i):
                                v = st[:, 0:128].rearrange("p (c j) -> p c j", c=2)
                                for j in range(2):
                                    P.op("pool", lambda j=j: nc.gpsimd.tensor_scalar(out=wuk[i][:, j, :], in0=v[:, j, :], scalar1=gkv[:, j:j + 1],
                                                                                     scalar2=None, op0=ALU.mult), reads=[sk, "gkv"], pwrites=[("wuk", i)])
                            wload(f"B_uq_{l}_{h}", None, None, cast=cast_uq)
                            wload(f"B_uk_{l}_{h}", None, None, cast=cast_uk)
                        load_head(0)
                        for h in range(8):
                            i = h % 2
                            if h + 1 < 8:
                                load_head(h + 1)
                            qT, qk = qTs[i], ("q", i)
                            kT, kk = kTs[i], ("k", i)
                            wq, wqk = wuq[i], ("wuq", i)
                            wk_, wkk = wuk[i], ("wuk", i)
                            for tt in range(NTT):
                                sl = slice(tt * 512, (tt + 1) * 512)
                                pb, pk = nextP()
                                pb2, pk2 = nextP()
                                for j in range(3):
                                    P.op("pe", lambda j=j, sl=sl, pb=pb, wq=wq: nc.tensor.matmul(pb[0:96, :], lhsT=wq[:, j, 0:96], rhs=cqT[:, j, sl],
                                                                                                 start=(j == 0), stop=(j == 2)), reads=[wqk, "cqT"], pwrites=[pk])
                                for j in range(3):
                                    P.op("pe", lambda j=j, sl=sl, pb2=pb2, wq=wq: nc.tensor.matmul(pb2[0:96, :], lhsT=wq[:, j, 32:128], rhs=cqT[:, j, sl],
                                                                                                   start=(j == 0), stop=(j == 2)), reads=[wqk, "cqT"], pwrites=[pk2])
                                P.op("act", lambda pb=pb, qT=qT, sl=sl: nc.scalar.copy(out=qT[0:64, sl], in_=pb[0:64, :]), reads=[pk], pwrites=[qk])
                                P.op("dve", lambda pb=pb, sl=sl: nc.vector.tensor_tensor(out=rt[0][64:96, :], in0=pb[64:96, :], in1=cosT[64:96, sl], op=ALU.mult),
                                     reads=[pk, "cosT"], writes=[("rt", 0)])
                                P.op("dve", lambda pb2=pb2, sl=sl: nc.vector.tensor_tensor(out=rt[1][64:96, :], in0=pb2[64:96, :], in1=sinT[64:96, sl], op=ALU.mult),
                                     reads=[pk2, "sinT"], writes=[("rt", 1)])
                                P.op("dve", lambda qT=qT, sl=sl: nc.vector.tensor_tensor(out=qT[64:96, sl], in0=rt[0][64:96, :], in1=rt[1][64:96, :], op=ALU.add),
                                     reads=[("rt", 0), ("rt", 1)], pwrites=[qk])
                            for tt in range(NTT):
                                pb, pk = nextP()
                                sl = slice(tt * 512, (tt + 1) * 512)
                                for j in range(2):
                                    P.op("pe", lambda j=j, pb=pb, wk_=wk_, sl=sl: nc.tensor.matmul(pb[0:64, :], lhsT=wk_[:, j, 0:64], rhs=ckvT[:, j, sl],
                                                                                                   start=(j == 0), stop=(j == 1)), reads=[wkk, "ckvT"], pwrites=[pk])
                                P.op("act", lambda pb=pb, kT=kT, sl=sl: nc.scalar.copy(out=kT[0:64, sl], in_=pb[0:64, :]), reads=[pk], pwrites=[kk])
                            attention(n, h, qT, qk, kT, kk, Vall[:, :, h, :], "vall", 96, "B", pTs=pTs, rc=rc, bcs=bcs, tmpo=tmpo, ybr=ybrs[n])
                        flush_pend()
                        barrier()

                if stage is None or stage >= 1:
                    plain_mixer(0, "A")
                if stage is None or stage >= 2:
                    mla_mixer(1)
                if stage is None or stage >= 3:
                    plain_mixer(2, "C")
                if stage is not None and stage <= 3:
                    dbs = ExitStack()
                    alloc_ln(dbs, 2, params=False)
                    xbuf = LN["xbuf"]
                    for n in range(stage):
                        for cc in range(4):
                            for q4 in range(2):
                                i = ctr["xb"] % 2
                                ctr["xb"] += 1
                                xb, xk = xbuf[i], ("xbuf", i)
                                P.op("dve", lambda xb=xb, n=n, cc=cc, q4=q4: nc.vector.tensor_copy(out=xb[:, :], in_=ybrs[n][:, cc, q4 * 1024:(q4 + 1) * 1024]),
                                     reads=[("y", n, 2 * q4), ("y", n, 2 * q4 + 1)], writes=[xk])
                                P.op("sp", lambda xb=xb, n=n, cc=cc, q4=q4: nc.sync.dma_start(out=dbg[n, cc, :, q4 * 1024:(q4 + 1) * 1024], in_=xb[:, :]),
                                     reads=[xk], dma=True)
                    barrier()
                    dbs.close()
                    break
                with ExitStack() as ph:
                    mergedT = sb(ph, "mergedT", [128, 8, T], BF16)
                    with ExitStack() as p1:
                        wm = [sb(p1, f"wm{i}", [128, 1536], BF16) for i in range(2)]
                        bg = sb(p1, "bg", [128, 24], F32)
                        macc = sb(p1, "macc", [128, T], F32)
                        sig = [sb(p1, f"sig{i}", [128, 512], F32) for i in range(3)]
                        vload(f"M_bg_{l}", bg[:], "bg")
                        cnt_m = 0
                        order = [(d, n) for d in range(8) for n in range(3)]
                        wload(f"M_{l}_0_0", wm[0][:, :], ("wm", 0), pw=False)
                        for oi_, (d, n) in enumerate(order):
                            i = oi_ % 2
                            if oi_ + 1 < len(order):
                                d2, n2 = order[oi_ + 1]
                                wload(f"M_{l}_{d2}_{n2}", wm[(oi_ + 1) % 2][:, :], ("wm", (oi_ + 1) % 2), pw=False)
                            w = wm[i]
                            wkey = ("wm", i)
                            wg = w[:, 0:1024].rearrange("p (c j) -> p c j", c=8)
                            wbr = w[:, 1024:1536].rearrange("p (c j) -> p c j", c=4)
                            for tt in range(NTT):
                                sl = slice(tt * 512, (tt + 1) * 512)
                                pg, pgk = nextP(big=True)
                                pp, ppk = nextP(big=True)
                                for c in range(8):
                                    P.op("pe", lambda c=c, pg=pg, wg=wg, sl=sl: nc.tensor.matmul(pg[:, :], lhsT=wg[:, c, :], rhs=xT[:, c, sl], start=(c == 0), stop=(c == 7)),
                                         reads=[wkey, ("xT", tt)], pwrites=[pgk])
                                for c in range(4):
                                    P.op("pe", lambda c=c, pp=pp, wbr=wbr, sl=sl, n=n: nc.tensor.matmul(pp[:, :], lhsT=wbr[:, c, :], rhs=ybrs[n][:, c, sl], start=(c == 0), stop=(c == 3)),
                                         reads=[wkey, ("y", n, tt)], pwrites=[ppk])
                                si = cnt_m % 3
                                cnt_m += 1
                                sg = sig[si]
                                P.op("act", lambda pg=pg, sg=sg, n=n, d=d: nc.scalar.activation(out=sg[:, :], in_=pg[:, :], func=AF.Sigmoid, bias=bg[:, n * 8 + d: n * 8 + d + 1], scale=1.0),
                                     reads=[pgk, "bg"], writes=[("sig", si)])
                                if n == 0:
                                    P.op("dve", lambda pp=pp, sg=sg, sl=sl: nc.vector.tensor_tensor(out=macc[:, sl], in0=pp[:, :], in1=sg[:, :], op=ALU.mult),
                                         reads=[ppk, ("sig", si)], writes=[("macc", tt)])
                                else:
                                    P.op("dve", lambda pp=pp, sg=sg: nc.vector.tensor_tensor(out=sg[:, :], in0=pp[:, :], in1=sg[:, :], op=ALU.mult),
                                         reads=[ppk, ("sig", si)], writes=[("sig", si)])
                                    if n == 1:
                                        P.op("pool", lambda sg=sg, sl=sl: nc.gpsimd.tensor_tensor(out=macc[:, sl], in0=macc[:, sl], in1=sg[:, :], op=ALU.add),
                                             reads=[("sig", si), ("macc", tt)], writes=[("macc", tt)])
                                    else:
                                        P.op("pool", lambda sg=sg, sl=sl, d=d: nc.gpsimd.tensor_tensor(out=mergedT[:, d, sl], in0=macc[:, sl], in1=sg[:, :], op=ALU.add),
                                             reads=[("sig", si), ("macc", tt)], pwrites=[("mT", tt)])
                        barrier()
                    wout = sb(ph, "wout", [128, 8, 1024], BF16)
                    alloc_ln(ph, 4)
                    vload(f"LN_mix_g_{l}", LN["lng"][:], "lng")
                    vload(f"LN_mix_b_{l}", LN["lnb"][:], "lnb")
                    for pc in range(4):
                        wload(f"M_out_{l}", wout[:, pc * 2:(pc + 1) * 2, :].rearrange("p c j -> p (c j)"), "wout", lo=pc * 2048, n=2048)

                    def mm_out(tb, hf, pb, pk):
                        for c in range(8):
                            P.op("pe", lambda c=c: nc.tensor.matmul(pb[:, :], lhsT=mergedT[:, c, tb * 128:(tb + 1) * 128], rhs=wout[:, c, hf * 512:(hf + 1) * 512],
                                                                    start=(c == 0), stop=(c == 7)), reads=[("mT", tb // 4), "wout"], pwrites=[pk])
                    outproj_ln(mm_out, xsrc, (xres[0], "xres0"))
                    barrier()
            if stage is not None and stage <= 3:
                break
            if stage == 4:
                for tb in range(NTB):
                    P.op("sp", lambda tb=tb: nc.sync.dma_start(out=dbg[tb * 128:(tb + 1) * 128, :], in_=xres[0][tb * 128:(tb + 1) * 128, :]),
                         reads=[("xd", "xres0", tb)], dma=True)
                break
            with ExitStack() as ph:
                wq = [sb(ph, f"wq{i}", [128, 8, 128], BF16) for i in range(2)]
                wk = [sb(ph, f"wk{i}", [128, 8, 128], BF16) for i in range(2)]
                wv = sb(ph, "wv", [128, 8, 512], BF16)
                wo = sb(ph, "wo", [128, 4, 1024], BF16)
                kTx = sb(ph, "kTx", [128, 4, 256], BF16)
                Vx = sb(ph, "Vx", [128, 2, 512], BF16)
                qTx = [sb(ph, f"qTx{i}", [128, T], BF16) for i in range(2)]
                oT = sb(ph, "oT", [128, 4, T], BF16)
                pTs = [sb(ph, f"pT{i}", [128, 512], BF16) for i in range(4)]
                rcx = [sb(ph, f"rcx{i}", [128, 512], F32) for i in range(2)]
                alloc_ln(ph, 4)
                vload(f"LN_xa_g_{l}", LN["lng"][:], "lng")
                vload(f"LN_xa_b_{l}", LN["lnb"][:], "lnb")
                for pc in range(2):
                    wload(f"X_v_{l}", wv[:, pc * 4:(pc + 1) * 4, :].rearrange("p c j -> p (c j)"), "wv", lo=pc * 2048, n=2048)
                for mb in range(2):
                    pb, pk = nextP()
                    for c in range(8):
                        P.op("pe", lambda c=c, mb=mb, pb=pb: nc.tensor.matmul(pb[:, :], lhsT=memT[:, c, mb * 128:(mb + 1) * 128], rhs=wv[:, c, :], start=(c == 0), stop=(c == 7)),
                             reads=["memT", "wv"], pwrites=[pk])
                    P.op("act", lambda mb=mb, pb=pb: nc.scalar.copy(out=Vx[:, mb, :], in_=pb[:, :]), reads=[pk], pwrites=["Vx"])

                def load_head(h):
                    i = h % 2
                    wload(f"X_q_{l}_{h}", wq[i][:, :, :].rearrange("p c j -> p (c j)"), ("wq", i), pw=False)
                    wload(f"X_k_{l}_{h}", wk[i][:, :, :].rearrange("p c j -> p (c j)"), ("wk", i), pw=False)
                load_head(0)
                for h in range(4):
                    i = h % 2
                    if h + 1 < 4:
                        load_head(h + 1)
                    pb, pk = nextP()
                    for c in range(8):
                        P.op("pe", lambda c=c, pb=pb, i=i: nc.tensor.matmul(pb[:, 0:256], lhsT=wk[i][:, c, :], rhs=memT[:, c, :], start=(c == 0), stop=(c == 7)),
                             reads=[("wk", i), "memT"], pwrites=[pk])
                    P.op("act", lambda pb=pb, h=h: nc.scalar.copy(out=kTx[:, h, :], in_=pb[:, 0:256]), reads=[pk], pwrites=[("kTx", h)])
                    qT, qk = qTx[i], ("qx", i)

                    def evq(tt, pb, pk, qT=qT, qk=qk):
                        P.op("dve", lambda: nc.vector.tensor_scalar(out=qT[:, tt * 512:(tt + 1) * 512], in0=pb[:, :], scalar1=128.0 ** -0.5, scalar2=None, op0=ALU.mult),
                             reads=[pk], pwrites=[qk])
                    proj_fm(wq[i], ("wq", i), 8, 0, 128, xT, lambda tt: ("xT", tt), evq)
                    for qt in range(NTT):
                        sl = slice(qt * 512, (qt + 1) * 512)
                        pis = []
                        for kb in range(2):
                            bi = ctr["s"] % 3
                            ctr["s"] += 1
                            P.op("pe", lambda kb=kb, bi=bi, sl=sl, h=h, qT=qT: nc.tensor.matmul(psS[bi][:, :], lhsT=kTx[:, h, kb * 128:(kb + 1) * 128], rhs=qT[:, sl], start=True, stop=True),
                                 reads=[("kTx", h), qk], pwrites=[("psS", bi)])
                            pi = ctr["pt"] % 4
                            ctr["pt"] += 1
                            pis.append(pi)
                            P.op("act", lambda bi=bi, pi=pi: nc.scalar.activation(out=pTs[pi][:, :], in_=psS[bi][:, :], func=AF.Exp), reads=[("psS", bi)], writes=[("pT", pi)])
                        oi = ctr["o"] % 2
                        ctr["o"] += 1
                        pb, pk = nextP()
                        for kb in range(2):
                            P.op("pe", lambda kb=kb, oi=oi, h=h, pi=pis[kb]: nc.tensor.matmul(psO[oi][:, :], lhsT=Vx[:, kb, h * 128:(h + 1) * 128], rhs=pTs[pi][:, :], start=(kb == 0), stop=(kb == 1)),
                                 reads=["Vx", ("pT", pis[kb])], pwrites=[("psO", oi)])
                        for kb in range(2):
                            P.op("pe", lambda kb=kb, pb=pb, pi=pis[kb]: nc.tensor.matmul(pb[:, :], lhsT=ones_b[:, :], rhs=pTs[pi][:, :], start=(kb == 0), stop=(kb == 1)),
                                 reads=["ones_b", ("pT", pis[kb])], pwrites=[pk])
                        ri = (h * 4 + qt) % 2
                        P.op("act", lambda pb=pb, ri=ri: nc.scalar.activation(out=rcx[ri][:, :], in_=pb[:, :], func=AF.Ln), reads=[pk], writes=[("rcx", ri)])
                        P.op("act", lambda ri=ri: nc.scalar.activation(out=rcx[ri][:, :], in_=rcx[ri][:, :], func=AF.Exp, scale=-1.0), reads=[("rcx", ri)], writes=[("rcx", ri)])
                        P.op("dve", lambda oi=oi, ri=ri, h=h, sl=sl: nc.vector.tensor_tensor(out=oT[:, h, sl], in0=psO[oi][:, :], in1=rcx[ri][:, :], op=ALU.mult),
                             reads=[("psO", oi), ("rcx", ri)], pwrites=[("oT", qt)])
                for pc in range(2):
                    wload(f"X_o_{l}", wo[:, pc * 2:(pc + 1) * 2, :].rearrange("p c j -> p (c j)"), "wo", lo=pc * 2048, n=2048)
                def mm_xo(tb, hf, pb, pk):
                    for c in range(4):
                        P.op("pe", lambda c=c: nc.tensor.matmul(pb[:, :], lhsT=oT[:, c, tb * 128:(tb + 1) * 128], rhs=wo[:, c, hf * 512:(hf + 1) * 512],
                                                                start=(c == 0), stop=(c == 3)), reads=[("oT", tb // 4), "wo"], pwrites=[pk])
                outproj_ln(mm_xo, (xres[0], "xres0"), (xres[1], "xres1"))
                barrier()
            if stage == 5:
                for tb in range(NTB):
                    P.op("sp", lambda tb=tb: nc.sync.dma_start(out=dbg[tb * 128:(tb + 1) * 128, :], in_=xres[1][tb * 128:(tb + 1) * 128, :]),
                         reads=[("xd", "xres1", tb)], dma=True)
                break

            with ExitStack() as ph:
                hT = sb(ph, "hT", [128, NJ, 1024], BF16)
                wgu = [sb(ph, f"wgu{i}", [128, 8, 256], BF16) for i in range(2)]
                wd = sb(ph, "wd", [128, NJ, 1024], BF16)
                sgb = [sb(ph, f"sgb{i}", [128, 512], F32) for i in range(3)]
                alloc_ln(ph, 4)
                vload(f"LN_ffn_g_{l}", LN["lng"][:], "lng")
                vload(f"LN_ffn_b_{l}", LN["lnb"][:], "lnb")
                xdst = (outd, "out") if l == DEPTH - 1 else (xres[2], "xres2")
                cg = 0
                for half in range(2):
                    wload(f"F_gu_{l}_0", wgu[0][:, :, :].rearrange("p c j -> p (c j)"), ("wgu", 0), pw=False)
                    for j in range(NJ):
                        i = j % 2
                        if j + 1 < NJ:
                            wload(f"F_gu_{l}_{j + 1}", wgu[(j + 1) % 2][:, :, :].rearrange("p c j -> p (c j)"), ("wgu", (j + 1) % 2), pw=False)
                        if half == 0 and j % 2 == 1:
                            jj = j // 2
                            wload(f"F_d_{l}_{jj}", wd[:, jj * 2:(jj + 1) * 2, :].rearrange("p c j -> p (c j)"), "wd")
                        for t2 in range(2):
                            tt = half * 2 + t2
                            sl = slice(tt * 512, (tt + 1) * 512)
                            pg, pgk = nextP(big=True)
                            pu, puk = nextP(big=True)
                            for c in range(8):
                                P.op("pe", lambda c=c, pg=pg, i=i, sl=sl: nc.tensor.matmul(pg[:, :], lhsT=wgu[i][:, c, 0:128], rhs=xT[:, c, sl], start=(c == 0), stop=(c == 7)),
                                     reads=[("wgu", i), ("xT", tt)], pwrites=[pgk])
                            for c in range(8):
                                P.op("pe", lambda c=c, pu=pu, i=i, sl=sl: nc.tensor.matmul(pu[:, :], lhsT=wgu[i][:, c, 128:256], rhs=xT[:, c, sl], start=(c == 0), stop=(c == 7)),
                                     reads=[("wgu", i), ("xT", tt)], pwrites=[puk])
                            si = cg % 3
                            cg += 1
                            P.op("act", lambda pg=pg, si=si: nc.scalar.activation(out=sgb[si][:, :], in_=pg[:, :], func=AF.Silu), reads=[pgk], writes=[("sgb", si)])
                            P.op("dve", lambda pu=pu, si=si, j=j, t2=t2: nc.vector.tensor_tensor(out=hT[:, j, t2 * 512:(t2 + 1) * 512], in0=pu[:, :], in1=sgb[si][:, :], op=ALU.mult),
                                 reads=[puk, ("sgb", si)], pwrites=[("hT", t2)])
                    def mm_dn(tb, hf, pb, pk, half=half):
                        t8 = tb
                        for j in range(NJ):
                            P.op("pe", lambda j=j: nc.tensor.matmul(pb[:, :], lhsT=hT[:, j, t8 * 128:(t8 + 1) * 128], rhs=wd[:, j, hf * 512:(hf + 1) * 512],
                                                                    start=(j == 0), stop=(j == NJ - 1)), reads=[("hT", t8 // 4), "wd"], pwrites=[pk])
                    outproj_ln_half(mm_dn, half, (xres[1], "xres1"), xdst)
                barrier()
            if stage == 6:
                for tb in range(NTB):
                    P.op("sp", lambda tb=tb: nc.sync.dma_start(out=dbg[tb * 128:(tb + 1) * 128, :], in_=xres[2][tb * 128:(tb + 1) * 128, :]),
                         reads=[("xd", "xres2", tb)], dma=True)
                break

        P.flush()
        for k, v in enumerate(P.dcnt):
            if v:
                nc.sync.wait_ge(dsems[k], v)
        print("ops", len(P.ops), "counts", P.cnt, "ndma", P.n_dma, flush=True)
    return nc


_NC_CACHE = {}


def kernel(**inputs):
    wp = pack_weights(inputs)
    if "nc" not in _NC_CACHE:
        _NC_CACHE["nc"] = build()
    nc = _NC_CACHE["nc"]
    x = np.asarray(inputs["x"], dtype=np.float32)
    mem = np.asarray(inputs["mem"], dtype=np.float32)
    pos = np.asarray(inputs["positions"]).astype(np.int32)
    in_maps = []
    for b in range(8):
        in_maps.append({"xin": np.ascontiguousarray(x[b]), "mem": np.ascontiguousarray(mem[b]),
                        "pos": np.ascontiguousarray(pos[b][None, :]), "wpack": wp})
    res = run_bass_kernel_spmd(nc, in_maps, core_ids=list(range(8)))
    out = np.stack([np.asarray(r["out"], dtype=np.float32) for r in res.results], axis=0)
    return out
```
